# Optimizing a Trainium2 kernel written in Bass

```python
import math, functools
import jax, jax.numpy as jnp
from jax import lax
import numpy as np

D_MODEL = 1024
BATCH = 4
SEQ = 4096
DEPTH = 2
DEC_BATCH = 32
DEC_SEQ = 8
PAST_LEN = 8192
PAGE_SIZE = 128

N_META = 16
NORM_EPS = 1e-6
SSM_EXPAND = 2
D_INNER = SSM_EXPAND * D_MODEL
SSM_HEAD_DIM = 64
SSM_HEADS = D_INNER // SSM_HEAD_DIM
SSM_GROUPS = 4
SSM_HPG = SSM_HEADS // SSM_GROUPS
SSM_STATE = 128
CONV_W = 4
CONV_DIM = D_INNER + 2 * SSM_GROUPS * SSM_STATE
SSM_CHUNK = 128
ATT_HEADS = 16
ATT_KV_HEADS = 4
ATT_GROUP = ATT_HEADS // ATT_KV_HEADS
HEAD_DIM = 64
ATT_DIM = ATT_HEADS * HEAD_DIM
KV_DIM = ATT_KV_HEADS * HEAD_DIM
IDX_HEADS = 8
IDX_DIM = 64
TOPK_MAX = 256
Q_BLOCK = 128
D_FF = 4 * D_MODEL

IN_SPLITS = (D_INNER, CONV_DIM, SSM_HEADS,
             ATT_DIM, KV_DIM, KV_DIM,
             IDX_HEADS * IDX_DIM, IDX_DIM, IDX_HEADS,
             D_MODEL, D_MODEL)
IN_DIM = D_INNER + CONV_DIM + SSM_HEADS + ATT_DIM + 2 * KV_DIM + IDX_HEADS * IDX_DIM + IDX_DIM + IDX_HEADS + 2 * D_MODEL

kernel_name = "hybrid_ssd_dsa_gated_decoder_step"

F32 = jnp.float32


def rmsnorm(x, w):
    xf = x.astype(F32)
    y = xf * lax.rsqrt(jnp.mean(xf * xf, axis=-1, keepdims=True) + NORM_EPS)
    return (y * w.astype(F32)).astype(x.dtype)


def split_in(p):
    offs = np.cumsum(IN_SPLITS)[:-1].tolist()
    return jnp.split(p, offs, axis=-1)


def causal_conv(xbc, prefix, w, bias):
    L = xbc.shape[1]
    xp = jnp.concatenate([prefix.astype(xbc.dtype), xbc], axis=1)
    y = bias + xp[:, :L] * w[0]
    for k in range(1, CONV_W):
        y = y + xp[:, k:k + L] * w[k]
    return jax.nn.silu(y), xp[:, L:]


def ssd_scan(x, dt, A, Bm, Cm, h0, chunk):
    b, L = x.shape[:2]
    nc = L // chunk

    def to_chunks(t):
        return jnp.moveaxis(t.reshape((b, nc, chunk) + t.shape[2:]), 1, 0)

    xs = (to_chunks(x.astype(F32).reshape(b, L, SSM_GROUPS, SSM_HPG, SSM_HEAD_DIM)),
          to_chunks(dt.astype(F32).reshape(b, L, SSM_GROUPS, SSM_HPG)),
          to_chunks(Bm.astype(F32)), to_chunks(Cm.astype(F32)))
    a_h = A.astype(F32).reshape(SSM_GROUPS, SSM_HPG)
    causal = jnp.tril(jnp.ones((chunk, chunk), bool))

    def step(h, inp):
        xc, dc, bc, cc = inp
        acum = jnp.cumsum(dc * a_h, axis=1)
        seg = acum[:, :, None] - acum[:, None, :]
        decay = jnp.exp(jnp.where(causal[None, :, :, None, None], seg, -jnp.inf))
        cb = jnp.einsum('bign,bjgn->bijg', cc, bc)
        xdt = xc * dc[..., None]
        y = jnp.einsum('bijg,bijge,bjgep->bigep', cb, decay, xdt)
        y = y + jnp.einsum('bign,bgepn->bigep', cc, h) * jnp.exp(acum)[..., None]
        last = acum[:, -1]
        wj = jnp.exp(last[:, None] - acum)
        h = h * jnp.exp(last)[..., None, None] + jnp.einsum('bjge,bjgep,bjgn->bgepn', wj, xdt, bc)
        return h, y

    h0g = h0.astype(F32).reshape(b, SSM_GROUPS, SSM_HPG, SSM_HEAD_DIM, SSM_STATE)
    hT, ys = lax.scan(step, h0g, xs)
    y = jnp.moveaxis(ys, 0, 1).reshape(b, L, SSM_HEADS, SSM_HEAD_DIM)
    return y, hT.reshape(b, SSM_HEADS, SSM_HEAD_DIM, SSM_STATE)


def ssm_mixer(z, xbc_raw, dt_raw, conv_prefix, h0, segments,
              conv_w, conv_b, dt_bias, a_log, d_skip, ssm_norm_w):
    b, L = z.shape[:2]
    xbc, conv_state = causal_conv(xbc_raw, conv_prefix, conv_w, conv_b)
    xs, Bm, Cm = jnp.split(xbc, [D_INNER, D_INNER + SSM_GROUPS * SSM_STATE], axis=-1)
    xs = xs.reshape(b, L, SSM_HEADS, SSM_HEAD_DIM)
    Bm = Bm.reshape(b, L, SSM_GROUPS, SSM_STATE)
    Cm = Cm.reshape(b, L, SSM_GROUPS, SSM_STATE)
    dt = jax.nn.softplus(dt_raw.astype(F32) + dt_bias.astype(F32))
    A = -jnp.exp(a_log.astype(F32))
    ys, h, start = [], h0, 0
    for length, chunk in segments:
        sl = slice(start, start + length)
        y, h = ssd_scan(xs[:, sl], dt[:, sl], A, Bm[:, sl], Cm[:, sl], h, chunk)
        ys.append(y)
        start += length
    y = jnp.concatenate(ys, axis=1) if len(ys) > 1 else ys[0]
    y = y + xs.astype(F32) * d_skip.astype(F32)[:, None]
    y = y.reshape(b, L, D_INNER) * jax.nn.silu(z.astype(F32))
    yg = y.reshape(b, L, SSM_GROUPS, D_INNER // SSM_GROUPS)
    yg = yg * lax.rsqrt(jnp.mean(yg * yg, axis=-1, keepdims=True) + NORM_EPS)
    y = yg.reshape(b, L, D_INNER) * ssm_norm_w.astype(F32)
    return y.astype(z.dtype), h, conv_state


def indexer_scores(qi, wi, ki):
    s = jax.nn.relu(jnp.einsum('bqhd,bsd->bqhs', qi.astype(F32), ki.astype(F32)) * IDX_DIM ** -0.5)
    return jnp.einsum('bqhs,bqh->bqs', s, wi.astype(F32) * IDX_HEADS ** -0.5)


def select_keys(scores, q_pos, topk):
    s_pos = jnp.arange(scores.shape[-1])
    admissible = s_pos[None, None, :] <= q_pos[None, :, None]
    _, idx = lax.top_k(jnp.where(admissible, scores, -jnp.inf), topk)
    valid = idx <= q_pos[None, :, None]
    return idx, valid


def attend_selected(q, k_sel, v_sel, valid):
    b, nq = q.shape[:2]
    qh = q.reshape(b, nq, ATT_KV_HEADS, ATT_GROUP, HEAD_DIM).astype(F32)
    s = jnp.einsum('bqhgd,bqkhd->bqhgk', qh, k_sel.astype(F32)) * HEAD_DIM ** -0.5
    s = jnp.where(valid[:, :, None, None, :], s, -jnp.inf)
    p = jax.nn.softmax(s, axis=-1)
    o = jnp.einsum('bqhgk,bqkhd->bqhgd', p, v_sel.astype(F32))
    return o.reshape(b, nq, ATT_DIM).astype(q.dtype)


def take_rows(t, i):
    return jax.vmap(lambda tt, ii: tt[ii])(t, i)


def prompt_sparse_attention(q, k, v, qi, wi, ki, topk):
    b, L = q.shape[:2]
    nblk = -(-L // Q_BLOCK)
    pad = nblk * Q_BLOCK - L

    def blocks(t):
        t = jnp.pad(t, [(0, 0), (0, pad)] + [(0, 0)] * (t.ndim - 2))
        return jnp.moveaxis(t.reshape((b, nblk, Q_BLOCK) + t.shape[2:]), 1, 0)

    k_h = k.reshape(b, L, ATT_KV_HEADS, HEAD_DIM)
    v_h = v.reshape(b, L, ATT_KV_HEADS, HEAD_DIM)

    def body(inp):
        qb, qib, wib, start = inp
        q_pos = start + jnp.arange(Q_BLOCK)
        idx, valid = select_keys(indexer_scores(qib, wib, ki), q_pos, topk)
        return attend_selected(qb, take_rows(k_h, idx), take_rows(v_h, idx), valid)

    starts = jnp.arange(nblk) * Q_BLOCK
    out = lax.map(body, (blocks(q), blocks(qi), blocks(wi), starts))
    return jnp.moveaxis(out, 0, 1).reshape(b, nblk * Q_BLOCK, ATT_DIM)[:, :L]


def sample_sparse_attention(q, k, v, qi, wi, ki, pool_k, pool_v, pool_ki, page_table, topk):
    bd, ns = q.shape[:2]
    past = page_table.shape[1] * PAGE_SIZE
    ki_past = pool_ki[page_table].reshape(bd, past, IDX_DIM)
    ki_all = jnp.concatenate([ki_past, ki.astype(ki_past.dtype)], axis=1)
    q_pos = past + jnp.arange(ns)
    idx, valid = select_keys(indexer_scores(qi, wi, ki_all), q_pos, topk)
    in_past = (idx < past)[..., None, None]
    past_idx = jnp.minimum(idx, past - 1)
    phys = take_rows(page_table, past_idx // PAGE_SIZE)
    off = past_idx % PAGE_SIZE
    new_idx = jnp.clip(idx - past, 0, ns - 1)
    k_new = k.reshape(bd, ns, ATT_KV_HEADS, HEAD_DIM)
    v_new = v.reshape(bd, ns, ATT_KV_HEADS, HEAD_DIM)
    k_sel = jnp.where(in_past, pool_k[phys, off], take_rows(k_new, new_idx))
    v_sel = jnp.where(in_past, pool_v[phys, off], take_rows(v_new, new_idx))
    return attend_selected(q, k_sel, v_sel, valid)


def hybrid_layer(h, conv_prefix, h0, segments, attend,
                 norm1_w, w_in, conv_w, conv_b, dt_bias, a_log, d_skip, ssm_norm_w,
                 w_ssm_proj, w_attn_proj, w_out, norm2_w, w_up, w_down):
    b, L = h.shape[:2]
    u = rmsnorm(h, norm1_w)
    z, xbc, dt_raw, q, k, v, qi, ki, wi, g_ssm, g_attn = split_in(u @ w_in)
    y_ssm, ssm_new, conv_new = ssm_mixer(z, xbc, dt_raw, conv_prefix, h0, segments,
                                         conv_w, conv_b, dt_bias, a_log, d_skip, ssm_norm_w)
    qi = qi.reshape(b, L, IDX_HEADS, IDX_DIM)
    y_att = attend(q, k, v, qi, wi, ki)
    merged = jax.nn.sigmoid(g_ssm) * (y_ssm @ w_ssm_proj) + jax.nn.sigmoid(g_attn) * (y_att @ w_attn_proj)
    h = h + merged @ w_out
    u2 = rmsnorm(h, norm2_w)
    h = h + jnp.square(jax.nn.relu(u2 @ w_up)) @ w_down
    rows = (k.reshape(b, L, ATT_KV_HEADS, HEAD_DIM), v.reshape(b, L, ATT_KV_HEADS, HEAD_DIM), ki, ssm_new, conv_new)
    return h, rows


def setup_inputs(seed: int = 0) -> dict:
    key = jax.random.key(seed)
    ks = jax.random.split(key, 32)
    n_pages = PAST_LEN // PAGE_SIZE
    n_pool = (DEC_BATCH * n_pages * 5) // 4

    def nrm(k, shape, scale):
        return jax.random.normal(k, shape, F32) * scale

    dt0 = jnp.exp(jax.random.uniform(ks[13], (DEPTH, SSM_HEADS), F32, math.log(1e-3), math.log(1e-1)))
    page_table = jax.random.permutation(ks[7], n_pool)[:DEC_BATCH * n_pages].reshape(DEC_BATCH, n_pages).astype(jnp.int32)
    return {
        "x_prompt": nrm(ks[0], (BATCH, SEQ, D_MODEL), 1.0),
        "x_sample": nrm(ks[1], (DEC_BATCH, DEC_SEQ, D_MODEL), 1.0),
        "cache_k": nrm(ks[2], (DEPTH, n_pool, PAGE_SIZE, ATT_KV_HEADS, HEAD_DIM), 1.0),
        "cache_v": nrm(ks[3], (DEPTH, n_pool, PAGE_SIZE, ATT_KV_HEADS, HEAD_DIM), 1.0),
        "cache_kidx": nrm(ks[4], (DEPTH, n_pool, PAGE_SIZE, IDX_DIM), 1.0),
        "state_ssm": nrm(ks[5], (DEPTH, DEC_BATCH, SSM_HEADS, SSM_HEAD_DIM, SSM_STATE), 0.3),
        "state_conv": nrm(ks[6], (DEPTH, DEC_BATCH, CONV_W - 1, CONV_DIM), 1.0),
        "page_table": page_table,
        "meta_tokens": nrm(ks[8], (N_META, D_MODEL), 1.0),
        "norm1_w": 1.0 + nrm(ks[9], (DEPTH, D_MODEL), 0.02),
        "w_in": nrm(ks[10], (DEPTH, D_MODEL, IN_DIM), D_MODEL ** -0.5),
        "conv_w": nrm(ks[11], (DEPTH, CONV_W, CONV_DIM), CONV_W ** -0.5),
        "conv_b": nrm(ks[12], (DEPTH, CONV_DIM), 0.02),
        "dt_bias": dt0 + jnp.log(-jnp.expm1(-dt0)),
        "a_log": jnp.log(jax.random.uniform(ks[14], (DEPTH, SSM_HEADS), F32, 1.0, 16.0)),
        "d_skip": 1.0 + nrm(ks[15], (DEPTH, SSM_HEADS), 0.02),
        "ssm_norm_w": 1.0 + nrm(ks[16], (DEPTH, D_INNER), 0.02),
        "w_ssm_proj": nrm(ks[17], (DEPTH, D_INNER, D_MODEL), D_INNER ** -0.5),
        "w_attn_proj": nrm(ks[18], (DEPTH, ATT_DIM, D_MODEL), ATT_DIM ** -0.5),
        "w_out": nrm(ks[19], (DEPTH, D_MODEL, D_MODEL), D_MODEL ** -0.5),
        "norm2_w": 1.0 + nrm(ks[20], (DEPTH, D_MODEL), 0.02),
        "w_up": nrm(ks[21], (DEPTH, D_MODEL, D_FF), D_MODEL ** -0.5),
        "w_down": nrm(ks[22], (DEPTH, D_FF, D_MODEL), D_FF ** -0.5),
        "final_norm_w": 1.0 + nrm(ks[23], (D_MODEL,), 0.02),
    }


def reference(x_prompt, x_sample, cache_k, cache_v, cache_kidx, state_ssm, state_conv, page_table,
              meta_tokens, norm1_w, w_in, conv_w, conv_b, dt_bias, a_log, d_skip, ssm_norm_w,
              w_ssm_proj, w_attn_proj, w_out, norm2_w, w_up, w_down, final_norm_w):
    b, seq = x_prompt.shape[:2]
    bd, ns = x_sample.shape[:2]
    past = page_table.shape[1] * PAGE_SIZE
    hp = jnp.concatenate([jnp.broadcast_to(meta_tokens[None].astype(x_prompt.dtype), (b, N_META, D_MODEL)), x_prompt], axis=1)
    hs = x_sample
    topk_p = min(TOPK_MAX, seq // 4)
    topk_s = min(TOPK_MAX, (past + ns) // 4)
    prompt_segments = ((N_META, N_META), (seq, SSM_CHUNK))
    sample_segments = ((ns, ns),)
    zero_prefix = jnp.zeros((b, CONV_W - 1, CONV_DIM), x_prompt.dtype)
    zero_state = jnp.zeros((b, SSM_HEADS, SSM_HEAD_DIM, SSM_STATE), F32)
    attend_p = functools.partial(prompt_sparse_attention, topk=topk_p)
    p_rows = [[], [], [], [], []]
    s_rows = [[], [], [], [], []]
    for l in range(DEPTH):
        weights = (norm1_w[l], w_in[l], conv_w[l], conv_b[l], dt_bias[l], a_log[l], d_skip[l], ssm_norm_w[l],
                   w_ssm_proj[l], w_attn_proj[l], w_out[l], norm2_w[l], w_up[l], w_down[l])
        hp, rows_p = hybrid_layer(hp, zero_prefix, zero_state, prompt_segments, attend_p, *weights)
        attend_s = functools.partial(sample_sparse_attention, pool_k=cache_k[l], pool_v=cache_v[l],
                                     pool_ki=cache_kidx[l], page_table=page_table, topk=topk_s)
        hs, rows_s = hybrid_layer(hs, state_conv[l], state_ssm[l], sample_segments, attend_s, *weights)
        for acc, r in zip(p_rows, rows_p):
            acc.append(r)
        for acc, r in zip(s_rows, rows_s):
            acc.append(r)
    y_prompt = rmsnorm(hp, final_norm_w)[:, N_META:]
    y_sample = rmsnorm(hs, final_norm_w)
    pk, pv, pki, pssm, pconv = [jnp.stack(r) for r in p_rows]
    sk, sv, ski, sssm, sconv = [jnp.stack(r) for r in s_rows]
    return (y_prompt, y_sample, pk, pv, pki, pssm, pconv, sk, sv, ski, sssm, sconv)
```

```python
import math
from contextlib import ExitStack
import numpy as np
import concourse.bass as bass
import concourse.mybir as mybir
from concourse.bass_utils import run_bass_kernel_spmd

F32 = mybir.dt.float32
BF16 = mybir.dt.bfloat16
I32 = mybir.dt.int32
U32 = mybir.dt.uint32
U8 = mybir.dt.uint8
AF = mybir.ActivationFunctionType
ALU = mybir.AluOpType
AX = mybir.AxisListType

D = 1024
N_META = 16
EPS = 1e-6
D_INNER = 2048
SSM_HEADS = 32
HD = 64
NSTATE = 128
CONV_DIM = 3072
KVD = 256
IDX_HEADS = 8
IDX_DIM = 64
D_FF = 4096
IN_DIM = 9320
PAGE = 128
C_Z, C_XBC, C_DT, C_Q, C_K, C_V, C_QI, C_KI, C_WI, C_GS, C_GA = 0, 2048, 5120, 5152, 6176, 6432, 6688, 7200, 7264, 7272, 8296
NEG = -30000.0
NBIS = 20


class Cfg:
    def __init__(self, seq=4096, nbs=4, dec_seq=8, n_pages=64, n_pool=2560, depth=2):
        self.seq = seq
        self.nbs = nbs
        self.ds = dec_seq
        self.n_pages = n_pages
        self.n_pool = n_pool
        self.depth = depth
        self.tp = N_META + seq
        self.ns = nbs * dec_seq
        self.tt = self.tp + self.ns
        self.topk_p = min(256, seq // 4)
        self.topk_s = min(256, (n_pages * PAGE + dec_seq) // 4)
        self.tiles = [(0, N_META)] + [(N_META + 128 * k, 128) for k in range(seq // 128)]
        self.past = n_pages * PAGE


class V:
    __slots__ = ("b", "ap")

    def __init__(self, b, ap):
        self.b = b
        self.ap = ap

    def bc(self, shape):
        return V(self.b, self.ap.to_broadcast(list(shape)))

    def us(self, axis):
        return V(self.b, self.ap.unsqueeze(axis))

    def r(self, pat, **kw):
        return V(self.b, self.ap.rearrange(pat, **kw))

    def __getitem__(self, k):
        return V(self.b, self.ap[k])


class Buf:
    __slots__ = ("t", "w", "r", "name", "free")

    def __init__(self, t, name="", free=False):
        self.t = t
        self.w = None
        self.r = []
        self.name = name
        self.free = free

    def __getitem__(self, k):
        return V(self, self.t[k])


class Prog:
    ENG = ("pe", "act", "dve", "pool", "sp")

    def __init__(self, nc, es, ring=20):
        self.nc = nc
        self.es = es
        self.e = {"pe": nc.tensor, "act": nc.scalar, "dve": nc.vector, "pool": nc.gpsimd, "sp": nc.sync}
        self.sem = {k: es.enter_context(nc.semaphore("s_" + k)) for k in self.ENG}
        self.epoch = {k: 0 for k in self.ENG}
        self.cnt = {k: 0 for k in self.ENG}
        self.total = {k: 0 for k in self.ENG}
        self.seen = {k: {} for k in self.ENG}
        self.ring = {q: [[es.enter_context(nc.semaphore(f"d_{q}{i}")), 0] for i in range(ring)] for q in ("sp", "pool")}
        self.rpos = {q: 0 for q in self.ring}
        self.ninst = 0
        self.cast_rr = 0

    def sb(self, name, shape, dt, es=None):
        self.uid = getattr(self, "uid", 0) + 1
        name = f"{name}_{self.uid}"
        return Buf((es or self.es).enter_context(self.nc.sbuf_tensor(name, list(shape), dt)), name)

    def ps(self, name, shape, dt=F32):
        return Buf(self.es.enter_context(self.nc.psum_tensor(name, list(shape), dt)), name)

    def dram(self, name, shape, dt, kind="Internal"):
        return Buf(self.nc.dram_tensor(name, list(shape), dt, kind=kind), name, free=True)

    def _wait(self, eng, tok):
        if tok is None:
            return
        sem, val, key = tok
        if self.seen[eng].get(key, 0) >= val:
            return
        self.e[eng].wait_ge(sem, val)
        self.seen[eng][key] = val

    def _deps(self, eng, reads, writes, acc):
        for b in reads:
            if not b.free:
                self._wait(eng, b.w)
        for b in writes:
            if b.free:
                continue
            if not (acc and b.w is not None and b.w[2].split("#")[0] == eng):
                self._wait(eng, b.w)
            for t in b.r:
                self._wait(eng, t)

    def _commit(self, tok, reads, writes):
        for b in reads:
            if not b.free:
                b.r = [t for t in b.r if t[2] != tok[2]] + [tok]
        for b in writes:
            if not b.free:
                b.w = tok
                b.r = []

    def op(self, eng, fn, reads=(), writes=(), acc=False):
        reads = [v.b if isinstance(v, V) else v for v in reads]
        writes = [v.b if isinstance(v, V) else v for v in writes]
        self._deps(eng, reads, writes, acc)
        ins = fn(self.e[eng])
        if self.cnt[eng] >= 30000:
            self.epoch[eng] += 1
            self.sem[eng] = self.es.enter_context(self.nc.semaphore(f"s_{eng}_{self.epoch[eng]}"))
            self.cnt[eng] = 0
        self.cnt[eng] += 1
        self.total[eng] += 1
        ins.then_inc(self.sem[eng], 1)
        tok = (self.sem[eng], self.cnt[eng], f"{eng}#{self.epoch[eng]}")
        self._commit(tok, reads, writes)
        self.ninst += 1
        return ins

    def dma(self, out, in_, q="sp", slow=False, fn=None, extra_reads=()):
        ring = self.ring[q]
        i = self.rpos[q]
        self.rpos[q] = (i + 1) % len(ring)
        slot = ring[i]
        key = f"d_{q}{i}"
        if slot[1] > 0:
            self._wait(q, (slot[0], slot[1], key))
        reads, writes = [in_.b] + [v.b if isinstance(v, V) else v for v in extra_reads], [out.b]
        self._deps(q, reads, writes, False)
        if fn is not None:
            ins = fn(self.e[q])
        elif slow:
            ins = self.e[q].dma_start(out=out.ap, in_=in_.ap, allow_slow_non_contiguous=True)
        else:
            ins = self.e[q].dma_start(out=out.ap, in_=in_.ap)
        slot[1] += 16
        ins.then_inc(slot[0], 16)
        tok = (slot[0], slot[1], key)
        self._commit(tok, reads, writes)
        self.ninst += 1
        return ins

    def barrier(self, bufs=()):
        toks = [(self.sem[k], self.cnt[k], f"{k}#{self.epoch[k]}") for k in self.ENG if self.cnt[k] > 0]
        for q, ring in self.ring.items():
            for i, slot in enumerate(ring):
                if slot[1] > 0:
                    toks.append((slot[0], slot[1], f"d_{q}{i}"))
        for eng in self.ENG:
            for t in toks:
                if t[2].split("#")[0] != eng:
                    self._wait(eng, t)

    def act(self, out, in_, func, bias=None, scale=None, accum=None, eng="act"):
        kw = {}
        reads = [in_]
        writes = [out]
        if bias is not None:
            if isinstance(bias, V):
                kw["bias"] = bias.ap
                reads.append(bias)
            else:
                kw["bias"] = bias
        if scale is not None:
            if isinstance(scale, V):
                kw["scale"] = scale.ap
                reads.append(scale)
            else:
                kw["scale"] = scale
        if accum is not None:
            kw["accum_out"] = accum.ap
            writes.append(accum)
        return self.op("act", lambda e: e.activation(out=out.ap, in_=in_.ap, func=func, **kw), reads, writes)

    def tt(self, out, in0, in1, op, eng="dve"):
        return self.op(eng, lambda e: e.tensor_tensor(out=out.ap, in0=in0.ap, in1=in1.ap, op=op), [in0, in1], [out])

    def ts(self, out, in0, s1, op0, s2=None, op1=None, accum=None, eng="dve"):
        reads = [in0]
        writes = [out]
        a1 = s1.ap if isinstance(s1, V) else s1
        a2 = s2.ap if isinstance(s2, V) else s2
        if isinstance(s1, V):
            reads.append(s1)
        if isinstance(s2, V):
            reads.append(s2)
        kw = {}
        if op1 is not None:
            kw["op1"] = op1
        if accum is not None:
            kw["accum_out"] = accum.ap
            writes.append(accum)
        return self.op(eng, lambda e: e.tensor_scalar(out=out.ap, in0=in0.ap, scalar1=a1, scalar2=a2, op0=op0, **kw), reads, writes)

    def stt(self, out, in0, s, in1, op0, op1):
        reads = [in0, in1]
        a = s.ap if isinstance(s, V) else s
        if isinstance(s, V):
            reads.append(s)
        return self.op("dve", lambda e: e.scalar_tensor_tensor(out=out.ap, in0=in0.ap, scalar=a, in1=in1.ap, op0=op0, op1=op1), reads, [out])

    def copy(self, out, in_, eng="dve"):
        if eng == "act":
            return self.act(out, in_, AF.Copy)
        return self.op(eng, lambda e: e.tensor_copy(out=out.ap, in_=in_.ap), [in_], [out])

    def memset(self, out, val, eng="pool"):
        return self.op(eng, lambda e: e.memset(out.ap, val), [], [out])

    def reduce(self, out, in_, op, axis=AX.X, absval=False):
        kw = {"apply_absolute_value": True} if absval else {}
        return self.op("dve", lambda e: e.tensor_reduce(out=out.ap, in_=in_.ap, axis=axis, op=op, **kw), [in_], [out])

    def recip(self, out, in_):
        return self.op("dve", lambda e: e.reciprocal(out=out.ap, in_=in_.ap), [in_], [out])

    def mm(self, out, lhsT, rhs, start=True, stop=True, skip=False):
        return self.op("pe", lambda e: e.matmul(out.ap, lhsT=lhsT.ap, rhs=rhs.ap, start=start, stop=stop, skip_group_check=skip),
                       [lhsT, rhs], [out], acc=True)

    def tr(self, out, in_, ident):
        return self.op("pe", lambda e: e.transpose(out=out.ap, in_=in_.ap, identity=ident.ap), [in_, ident], [out], acc=True)


def build(cfg):
    nc = bass.Bass("TRN2", target_bir_lowering=False)
    es = ExitStack()
    p = Prog(nc, es)
    TT, TP, NS = cfg.tt, cfg.tp, cfg.ns
    L = cfg.depth
    NT = len(cfg.tiles)
    def din(name, shape, dt=F32):
        return p.dram(name, shape, dt, kind="ExternalInput")

    def dout(name, shape):
        return p.dram(name, shape, F32, kind="ExternalOutput")

    xp = din("xp", [cfg.seq, D])
    xs = din("xs", [NS, D])
    meta = din("meta_tokens", [N_META, D])
    cache_k = din("cache_k", [L * cfg.n_pool * PAGE, KVD])
    cache_v = din("cache_v", [L * cfg.n_pool * PAGE, KVD])
    cache_ki = din("cache_kidx", [L * cfg.n_pool * PAGE, IDX_DIM])
    st_ssm = din("state_ssm", [L, cfg.nbs, D_INNER, NSTATE])
    st_conv = din("state_conv", [L, cfg.nbs, 3, CONV_DIM])
    ptab = din("page_table", [cfg.nbs, cfg.n_pages], I32)
    norm1_w = din("norm1_w", [L, D])
    w_in = din("w_in", [L, D, IN_DIM])
    conv_w = din("conv_w", [L, 4, CONV_DIM])
    conv_b = din("conv_b", [L, CONV_DIM])
    dt_bias = din("dt_bias", [L, SSM_HEADS])
    a_log = din("a_log", [L, SSM_HEADS])
    d_skip = din("d_skip", [L, SSM_HEADS])
    ssm_norm_w = din("ssm_norm_w", [L, D_INNER])
    w_ssm = din("w_ssm_proj", [L, D_INNER, D])
    w_attn = din("w_attn_proj", [L, D, D])
    w_out = din("w_out", [L, D, D])
    norm2_w = din("norm2_w", [L, D])
    w_up = din("w_up", [L, D, D_FF])
    w_down = din("w_down", [L, D_FF, D])
    fnorm_w = din("final_norm_w", [1, D])

    y_p = dout("y_p", [cfg.seq, D])
    y_s = dout("y_s", [NS, D])
    pk = dout("pk", [L, TP, KVD])
    pv = dout("pv", [L, TP, KVD])
    pki = dout("pki", [L, TP, IDX_DIM])
    pssm = dout("pssm", [L, D_INNER, NSTATE])
    pconv = dout("pconv", [L, 3, CONV_DIM])
    sk = dout("sk", [L, NS, KVD])
    sv = dout("sv", [L, NS, KVD])
    ski = dout("ski", [L, NS, IDX_DIM])
    sssm = dout("sssm", [L, cfg.nbs, D_INNER, NSTATE])
    sconv = dout("sconv", [L, cfg.nbs, 3, CONV_DIM])
    outs = [y_p, y_s, pk, pv, pki, pssm, pconv, sk, sv, ski, sssm, sconv]

    H = [None] + [p.dram(f"H{l}", [TT, D], F32) for l in range(1, L)]
    HM = p.dram("HM", [TT, D], F32)
    PTM = p.dram("PTM", [TT, IN_DIM], F32)
    XBT = p.dram("XBT", [CONV_DIM, TT], BF16)
    MS = p.dram("MS", [TT, D], F32)
    U2T = p.dram("U2T", [8, 128, TT], BF16)
    YAS = p.dram("YAS", [NS, D], F32)

    identf = p.sb("identf", [128, 128], F32)
    identb = p.sb("identb", [128, 128], BF16)
    tri = p.sb("tri", [128, 128], F32)
    cmask = p.sb("cmask", [128, 128], F32)
    onesf = p.sb("onesf", [128, 128], F32)
    onesb = p.sb("onesb", [1, 512], BF16)
    zerob = p.sb("zerob", [1, 512], BF16)
    pow2 = p.sb("pow2", [128, NBIS], F32)
    sel4 = p.sb("sel4", [4, 16], F32)
    PS = [p.ps(f"ps{i}", [128, 512], F32) for i in range(8)]

    p.memset(identf[:], 1.0)
    p.op("pool", lambda e: e.affine_select(out=identf.t[:], in_=identf.t[:], pattern=[[-1, 128]], compare_op=ALU.is_equal, fill=0.0, base=0, channel_multiplier=1), [identf], [identf])
    p.copy(identb[:], identf[:], eng="pool")
    p.memset(tri[:], 1.0)
    p.op("pool", lambda e: e.affine_select(out=tri.t[:], in_=tri.t[:], pattern=[[1, 128]], compare_op=ALU.is_ge, fill=0.0, base=0, channel_multiplier=-1), [tri], [tri])
    p.memset(cmask[:], 0.0)
    p.op("pool", lambda e: e.affine_select(out=cmask.t[:], in_=cmask.t[:], pattern=[[-1, 128]], compare_op=ALU.is_ge, fill=-1e30, base=0, channel_multiplier=1), [cmask], [cmask])
    p.memset(onesf[:], 1.0)
    p.memset(onesb[:], 1.0)
    p.memset(zerob[:], 0.0)
    for k in range(NBIS):
        p.memset(pow2[:, k:k + 1], 2.0 ** -(k + 1))
    p.copy(sel4[:].r("p (a b) -> p a b", a=4), identf[:4, :4].us(2).bc([4, 4, 4]), eng="pool")

    rr = {"ps": 0}

    def transpose_to(dst_fn, src, n_rows, widths, evac_eng="act", scale=None):
        col = 0
        blocks = []
        for w in widths:
            blocks.append((col, w))
            col += w
        for b0 in range(0, len(blocks), 4):
            grp = blocks[b0:b0 + 4]
            bank = PS[3 + (rr["ps"] % 2)]
            rr["ps"] += 1
            pv3 = bank[:].r("p (a b) -> p a b", a=4)
            for gi, (c0, w) in enumerate(grp):
                p.tr(pv3[:w, gi, :n_rows], src[:n_rows, c0:c0 + w], identf[:n_rows, :n_rows])
            dst_fn(b0, len(grp), pv3, grp)

    def load_w_bf16(dst, src_ap_fn, KC, N, stage_bufs):
        i = 0
        for kc in range(KC):
            for n0 in range(0, N, 1024):
                w = min(1024, N - n0)
                st = stage_bufs[i % len(stage_bufs)]
                i += 1
                p.dma(st[:, :w], src_ap_fn(kc, n0, w))
                eng = ("pool", "act", "dve")[p.cast_rr % 3]
                p.cast_rr += 1
                p.copy(dst[:, kc, n0:n0 + w], st[:, :w], eng=eng)

    def bcast_row(dst, src_row_v, q="sp"):
        p.dma(dst, V(src_row_v.b, src_row_v.ap.partition_broadcast(128)), q=q)

    def rmsnorm_tile(x, n, wb, out, ssq, tmp):
        p.act(tmp[:n, :], x[:n, :], AF.Square, accum=ssq[:n, 0:1])
        p.ts(ssq[:n, 1:2], ssq[:n, 0:1], 1.0 / D, ALU.mult, EPS, ALU.add)
        p.act(ssq[:n, 1:2], ssq[:n, 1:2], AF.Sqrt)
        p.recip(ssq[:n, 1:2], ssq[:n, 1:2])
        p.stt(out[:n, :], x[:n, :], ssq[:n, 1:2], wb[:n, :], ALU.mult, ALU.mult)

    def h_rows(l, r0, n):
        if l > 0:
            return [(H[l][r0:r0 + n, :], 0, n)]
        segs = []
        for (a, b, src, off) in ((0, N_META, meta, 0), (N_META, TP, xp, N_META), (TP, TT, xs, TP)):
            lo, hi = max(a, r0), min(b, r0 + n)
            if lo < hi:
                segs.append((src[lo - off:hi - off, :], lo - r0, hi - lo))
        return segs

    all_tiles = list(cfg.tiles) + [(TP, NS)]

    for l in range(L):
        with ExitStack() as ph:
            uT = p.sb("uT", [128, 8, TT], BF16, ph)
            n1b = p.sb("n1b", [128, D], F32, ph)
            bcast_row(n1b[:], norm1_w[l])
            xt = [p.sb(f"xt{i}", [128, D], F32, ph) for i in range(2)]
            ut = [p.sb(f"ut{i}", [128, D], F32, ph) for i in range(2)]
            sq = p.sb("sq", [128, D], F32, ph)
            ssq = [p.sb(f"ssq{i}", [128, 2], F32, ph) for i in range(2)]
            for ti, (r0, n) in enumerate(all_tiles):
                x = xt[ti % 2]
                u = ut[ti % 2]
                for (src, po, cnt) in h_rows(l, r0, n):
                    p.dma(x[po:po + cnt, :], src)
                rmsnorm_tile(x, n, n1b, u, ssq[ti % 2], sq)

                def ev(b0, cntb, pv3, grp, r0=r0, n=n):
                    p.copy(uT[:, b0:b0 + cntb, r0:r0 + n], pv3[:, :cntb, :n], eng="act" if (b0 // 4) % 2 == 0 else "dve")
                transpose_to(ev, u, n, [128] * 8)

            wst = [p.sb(f"wst{i}", [128, 8, 512], F32, ph) for i in range(2)]
            wbf = [p.sb(f"wbf{i}", [128, 8, 512], BF16, ph) for i in range(2)]
            ev32 = [p.sb(f"ev32_{i}", [128, 512], F32, ph) for i in range(3)]
            evbf = [p.sb(f"evbf{i}", [128, 512], BF16, ph) for i in range(3)]
            chunks = []
            for (c0, wd, kind) in ((C_Z, 2048, "tm"), (C_XBC, 3072, "fm"), (C_DT, 32, "tm"), (C_Q, 1024, "tm"), (C_K, 256, "tm"),
                                   (C_V, 256, "tm"), (C_QI, 512, "tm"), (C_KI, 64, "tm"), (C_WI, 8, "tm"), (C_GS, 1024, "tm"), (C_GA, 1024, "tm")):
                for o in range(0, wd, 512):
                    chunks.append((c0 + o, min(512, wd - o), kind))
            cnt_ev = 0
            for ci, (c0, wd, kind) in enumerate(chunks):
                st = wst[ci % 2]
                wb = wbf[ci % 2]
                p.dma(st[:, :, :wd], w_in[l, :, c0:c0 + wd].r("(kc p) n -> p kc n", p=128))
                p.copy(wb[:, :, :wd], st[:, :, :wd], eng=("pool", "dve")[ci % 2])
                if kind == "fm":
                    for cc in range(wd // 128):
                        crow = c0 - C_XBC + cc * 128
                        for t0 in range(0, TT, 512):
                            tw = min(512, TT - t0)
                            bank = PS[cnt_ev % 3]
                            for kc in range(8):
                                p.mm(bank[:, :tw], wb[:, kc, cc * 128:(cc + 1) * 128], uT[:, kc, t0:t0 + tw], start=(kc == 0), stop=(kc == 7))
                            eb = evbf[cnt_ev % 3]
                            p.copy(eb[:, :tw], bank[:, :tw], eng="act" if cnt_ev % 2 == 0 else "dve")
                            p.dma(XBT[crow:crow + 128, t0:t0 + tw], eb[:, :tw], q="pool")
                            cnt_ev += 1
                    tm_tiles = [all_tiles[NT - 1], all_tiles[NT]]
                else:
                    tm_tiles = all_tiles
                for (r0, n) in tm_tiles:
                    bank = PS[cnt_ev % 3]
                    for kc in range(8):
                        p.mm(bank[:n, :wd], uT[:, kc, r0:r0 + n], wb[:, kc, :wd], start=(kc == 0), stop=(kc == 7))
                    eb = ev32[cnt_ev % 3]
                    p.copy(eb[:n, :wd], bank[:n, :wd], eng="act" if cnt_ev % 2 == 0 else "dve")
                    p.dma(PTM[r0:r0 + n, c0:c0 + wd], eb[:n, :wd], q="pool")
                    for (cs, o_p, o_s) in ((C_K, pk, sk), (C_V, pv, sv), (C_KI, pki, ski)):
                        if c0 == cs:
                            if r0 < TP:
                                p.dma(o_p[l, r0:r0 + n, :], eb[:n, :wd], q="pool")
                            else:
                                p.dma(o_s[l, :, :], eb[:n, :wd], q="pool")
                    cnt_ev += 1
            p.barrier()
            p.dma(pconv[l, :, :], PTM[TP - 3:TP, C_XBC:C_XBC + CONV_DIM], q="sp")
            for b in range(cfg.nbs):
                r = TP + b * cfg.ds + cfg.ds - 3
                p.dma(sconv[l, b, :, :], PTM[r:r + 3, C_XBC:C_XBC + CONV_DIM], q="sp")
        p.barrier()

        with ExitStack() as ph:
            wssm = p.sb("wssm", [128, 16, D], BF16, ph)
            stg = [p.sb(f"stg{i}", [128, 1024], F32, ph) for i in range(2)]
            load_w_bf16(wssm, lambda kc, n0, w: w_ssm[l, kc * 128:(kc + 1) * 128, n0:n0 + w], 16, D, stg)
            wcol = p.sb("wcol", [128, 24, 4], F32, ph)
            for k in range(4):
                p.dma(wcol[:, :, k], conv_w[l, k].r("(cc p) -> p cc", p=128), slow=True)
            bcol = p.sb("bcol", [128, 24], F32, ph)
            p.dma(bcol[:], conv_b[l:l + 1, :].r("o (cc p) -> p (o cc)", p=128), slow=True)
            browb = p.sb("browb", [1, CONV_DIM], BF16, ph)
            for o in range(0, CONV_DIM, 1024):
                p.dma(stg[0][:1, :], conv_b[l:l + 1, o:o + 1024])
                p.copy(browb[:1, o:o + 1024], stg[0][:1, :])
            cdiag = p.sb("cdiag", [128, 96, 128], BF16, ph)
            for cc in range(24):
                for k in range(4):
                    p.ts(cdiag[:, cc * 4 + k, :], identf[:], wcol[:, cc, k:k + 1], ALU.mult, eng="dve" if (cc + k) % 2 else "pool")
            A_b = p.sb("A_b", [128, 32], F32, ph)
            bcast_row(A_b[:], a_log[l])
            p.act(A_b[:], A_b[:], AF.Exp)
            p.ts(A_b[:], A_b[:], -1.0, ALU.mult)
            dtb_b = p.sb("dtb_b", [128, 32], F32, ph)
            bcast_row(dtb_b[:], dt_bias[l])
            D32 = p.sb("D32", [128, 32], F32, ph)
            bcast_row(D32[:], d_skip[l])
            D_b = p.sb("D_b", [128, 32, 64], F32, ph)
            p.copy(D_b[:], D32[:].us(2).bc([128, 32, 64]))
            snw_b = p.sb("snw_b", [128, D_INNER], F32, ph)
            bcast_row(snw_b[:], ssm_norm_w[l])
            hT = p.sb("hT", [128, D_INNER], F32, ph)
            hTb = p.sb("hTb", [128, D_INNER], BF16, ph)
            xw = p.sb("xw", [128, 24, 131], BF16, ph)
            pre32 = p.sb("pre32", [128, 24, 3], F32, ph)
            x_tm = p.sb("x_tm", [128, D_INNER], F32, ph)
            B_tm = p.sb("B_tm", [128, 512], BF16, ph)
            BT = p.sb("BT", [128, 4, 128], BF16, ph)
            CT = p.sb("CT", [128, 4, 128], BF16, ph)
            sm = p.sb("sm", [128, 8, 32], F32, ph)
            dtAb = p.sb("dtAb", [128, 32, 128], F32, ph)
            explast = p.sb("explast", [128, 32], F32, ph)
            xdt = p.sb("xdt", [128, D_INNER], BF16, ph)
            xdtw = p.sb("xdtw", [128, D_INNER], BF16, ph)
            cbT = p.sb("cbT", [128, 4, 128], F32, ph)
            arg = [p.sb(f"arg{i}", [128, 4, 128], F32, ph) for i in range(2)]
            GT = [p.sb(f"GT{i}", [128, 4, 128], BF16, ph) for i in range(2)]
            zt = p.sb("zt", [128, D_INNER], F32, ph)
            yt = p.sb("yt", [128, D_INNER], F32, ph)
            t1 = p.sb("t1", [128, 512], F32, ph)
            gss = p.sb("gss", [128, 8], F32, ph)
            yT = p.sb("yT", [128, 16, 128], BF16, ph)
            gs = p.sb("gs", [128, D], F32, ph)
            mso = p.sb("mso", [128, D], F32, ph)
            st_t = p.sb("st_t", [128, 128], F32, ph)

            def ssd_chunk(r0, n, first, prefix_src):
                if first and prefix_src is None:
                    p.memset(xw[:, :, 0:3], 0.0)
                    p.dma(xw[:, :, 3:3 + n], XBT[:, r0:r0 + n].r("(cc p) t -> p cc t", p=128), slow=True)
                elif prefix_src is not None:
                    for r3 in range(3):
                        p.dma(pre32[:, :, r3], prefix_src[r3].r("(cc p) -> p cc", p=128), slow=True)
                    p.copy(xw[:, :, 0:3], pre32[:])
                    p.dma(xw[:, :, 3:3 + n], XBT[:, r0:r0 + n].r("(cc p) t -> p cc t", p=128), slow=True)
                else:
                    p.dma(xw[:, :, 0:3 + n], XBT[:, r0 - 3:r0 + n].r("(cc p) t -> p cc t", p=128), slow=True)
                p.dma(zt[:n, :], PTM[r0:r0 + n, C_Z:C_Z + D_INNER])
                p.dma(sm[:n, 0, :], PTM[r0:r0 + n, C_DT:C_DT + 32])
                p.dma(gs[:n, :], PTM[r0:r0 + n, C_GS:C_GS + D])
                for bk in range(4):
                    bank = PS[bk % 3]
                    for c4 in range(4):
                        cc = bk * 4 + c4
                        for k in range(4):
                            p.mm(bank[:n, c4 * 128:(c4 + 1) * 128], xw[:, cc, k:k + n], cdiag[:, cc * 4 + k, :], start=(k == 0), stop=False)
                        p.mm(bank[:n, c4 * 128:(c4 + 1) * 128], onesb[:1, :n], browb[:1, cc * 128:(cc + 1) * 128], start=False, stop=True)
                    p.act(x_tm[:n, bk * 512:(bk + 1) * 512], bank[:n, :], AF.Silu)
                bank = PS[1]
                for c4 in range(4):
                    cc = 16 + c4
                    for k in range(4):
                        p.mm(bank[:n, c4 * 128:(c4 + 1) * 128], xw[:, cc, k:k + n], cdiag[:, cc * 4 + k, :], start=(k == 0), stop=False)
                    p.mm(bank[:n, c4 * 128:(c4 + 1) * 128], onesb[:1, :n], browb[:1, cc * 128:(cc + 1) * 128], start=False, stop=True)
                p.act(B_tm[:n, :], bank[:n, :], AF.Silu)
                for which, dst in ((16, BT), (20, CT)):
                    bank = PS[2] if which == 16 else PS[0]
                    for g in range(4):
                        cc = which + g
                        for k in range(4):
                            p.mm(bank[:, g * 128:g * 128 + n], cdiag[:, cc * 4 + k, :], xw[:, cc, k:k + n], start=(k == 0), stop=(k == 3))
                    for g in range(4):
                        p.act(dst[:, g, :n], bank[:, g * 128:g * 128 + n], AF.Silu, bias=bcol[:, which + g:which + g + 1])
                dtraw, dt, dtA, acum, Ee, wj, dtw, tmp = [sm[:n, i, :] for i in range(8)]
                p.tt(dt, dtraw, dtb_b[:n, :], ALU.add)
                p.act(dt, dt, AF.Exp)
                p.act(dt, dt, AF.Ln, bias=1.0)
                p.tt(dtA, dt, A_b[:n, :], ALU.mult)
                bank = PS[3]
                p.mm(bank[:n, 0:32], tri[:n, :n], dtA, start=True, stop=True)
                p.mm(bank[:, 32:64], onesf[:n, :], dtA, start=True, stop=True)
                p.copy(acum, bank[:n, 0:32])
                p.act(Ee, bank[:n, 0:32], AF.Exp)
                p.act(explast[:], bank[:, 32:64], AF.Exp)
                p.tt(tmp, bank[:n, 32:64], acum, ALU.subtract)
                p.act(wj, tmp, AF.Exp)
                p.tt(dtw, dt, wj, ALU.mult)
                x3 = x_tm[:n, :].r("p (h d) -> p h d", h=32)
                p.tt(xdt[:n, :].r("p (h d) -> p h d", h=32), x3, dt.us(2).bc([n, 32, 64]), ALU.mult)
                p.tt(xdtw[:n, :].r("p (h d) -> p h d", h=32), x3, dtw.us(2).bc([n, 32, 64]), ALU.mult, eng="pool")
                p.copy(dtAb[:n, :, :n], dtA.us(2).bc([n, 32, n]), eng="pool")
                bank = PS[4]
                for g in range(4):
                    p.mm(bank[:n, g * 128:g * 128 + n], BT[:, g, :n], CT[:, g, :n], start=True, stop=True)
                p.tt(cbT[:n, :, :n], bank[:n, :].r("p (g i) -> p g i", g=4)[:, :, :n], tri[:n, :n].us(1).bc([n, 4, n]), ALU.mult)
                for g in range(4):
                    ybank = PS[5 + (g % 2)]
                    for q4 in range(2):
                        ab = PS[(g * 2 + q4) % 3]
                        ar = arg[(g * 2 + q4) % 2]
                        gt = GT[(g * 2 + q4) % 2]
                        for hh in range(4):
                            h = g * 8 + q4 * 4 + hh
                            p.mm(ab[:n, hh * 128:hh * 128 + n], dtAb[:n, h, :n], tri[:n, :n], start=True, stop=True)
                        for hh in range(4):
                            h = g * 8 + q4 * 4 + hh
                            p.ts(ar[:n, hh, :n], ab[:n, hh * 128:hh * 128 + n], acum[:, h:h + 1], ALU.subtract, 0.0, ALU.min)
                        p.act(ar[:n, :, :n], ar[:n, :, :n], AF.Exp)
                        p.tt(gt[:n, :, :n], ar[:n, :, :n], cbT[:n, g:g + 1, :n].bc([n, 4, n]), ALU.mult)
                        for hh in range(4):
                            h = g * 8 + q4 * 4 + hh
                            p.mm(ybank[:n, (q4 * 4 + hh) * 64:(q4 * 4 + hh + 1) * 64], gt[:n, hh, :n], xdt[:n, h * 64:(h + 1) * 64], start=True, stop=True)
                    ibank = PS[7]
                    p.mm(ibank[:n, :], CT[:, g, :n], hTb[:, g * 512:(g + 1) * 512], start=True, stop=True)
                    p.tt(t1[:n, :].r("p (h d) -> p h d", h=8), ibank[:n, :].r("p (h d) -> p h d", h=8), Ee[:, g * 8:(g + 1) * 8].us(2).bc([n, 8, 64]), ALU.mult)
                    p.tt(t1[:n, :], ybank[:n, :], t1[:n, :], ALU.add)
                    p.tt(yt[:n, g * 512:(g + 1) * 512], x_tm[:n, g * 512:(g + 1) * 512], D_b[:n, g * 8:(g + 1) * 8, :].r("p h d -> p (h d)"), ALU.mult, eng="pool")
                    p.tt(yt[:n, g * 512:(g + 1) * 512], yt[:n, g * 512:(g + 1) * 512], t1[:n, :], ALU.add)
                for g in range(4):
                    sbank = PS[(g + 1) % 3]
                    p.mm(sbank[:, :], B_tm[:n, g * 128:(g + 1) * 128], xdtw[:n, g * 512:(g + 1) * 512], start=True, stop=True)
                    hv = hT[:, g * 512:(g + 1) * 512]
                    p.tt(hv.r("p (h d) -> p h d", h=8), hv.r("p (h d) -> p h d", h=8), explast[:, g * 8:(g + 1) * 8].us(2).bc([128, 8, 64]), ALU.mult)
                    p.tt(hv, hv, sbank[:, :], ALU.add)
                    p.copy(hTb[:, g * 512:(g + 1) * 512], hv, eng="pool")
                p.act(zt[:n, :], zt[:n, :], AF.Silu)
                p.tt(yt[:n, :], yt[:n, :], zt[:n, :], ALU.mult)
                for g in range(4):
                    p.act(zt[:n, g * 512:(g + 1) * 512], yt[:n, g * 512:(g + 1) * 512], AF.Square, accum=gss[:n, g:g + 1])
                p.ts(gss[:n, 4:8], gss[:n, 0:4], 1.0 / 512, ALU.mult, EPS, ALU.add)
                p.act(gss[:n, 4:8], gss[:n, 4:8], AF.Sqrt)
                p.recip(gss[:n, 4:8], gss[:n, 4:8])
                for g in range(4):
                    p.stt(yt[:n, g * 512:(g + 1) * 512], yt[:n, g * 512:(g + 1) * 512], gss[:n, 4 + g:5 + g], snw_b[:n, g * 512:(g + 1) * 512], ALU.mult, ALU.mult)

                def ev(b0, cntb, pv3, grp):
                    p.copy(yT[:, b0:b0 + cntb, :n], pv3[:, :cntb, :n], eng="act" if (b0 // 4) % 2 == 0 else "dve")
                transpose_to(ev, yt, n, [128] * 16)
                p.act(gs[:n, :], gs[:n, :], AF.Sigmoid)
                for c2 in range(2):
                    bank = PS[5 + c2]
                    for kc in range(16):
                        p.mm(bank[:n, :], yT[:, kc, :n], wssm[:, kc, c2 * 512:(c2 + 1) * 512], start=(kc == 0), stop=(kc == 15))
                    p.tt(mso[:n, c2 * 512:(c2 + 1) * 512], bank[:n, :], gs[:n, c2 * 512:(c2 + 1) * 512], ALU.mult)
                p.dma(MS[r0:r0 + n, :], mso[:n, :], q="pool")

            def state_out(dst):
                for kc in range(16):
                    bank = PS[3 + kc % 2]
                    p.tr(bank[:, :128], hT[:, kc * 128:(kc + 1) * 128], identf[:])
                    p.copy(st_t[:], bank[:, :128], eng="act")
                    p.dma(dst[kc * 128:(kc + 1) * 128, :], st_t[:], q="pool")

            def state_in(src):
                for kc in range(16):
                    p.dma(st_t[:], src[kc * 128:(kc + 1) * 128, :])
                    bank = PS[3 + kc % 2]
                    p.tr(bank[:, :128], st_t[:], identf[:])
                    p.copy(hT[:, kc * 128:(kc + 1) * 128], bank[:, :128], eng="act")
                p.copy(hTb[:], hT[:])

            p.memset(hT[:], 0.0)
            p.memset(hTb[:], 0.0)
            for ti, (r0, n) in enumerate(cfg.tiles):
                ssd_chunk(r0, n, ti == 0, None)
            state_out(pssm[l])
            for b in range(cfg.nbs):
                state_in(st_ssm[l, b])
                ssd_chunk(TP + b * cfg.ds, cfg.ds, True, st_conv[l, b])
                state_out(sssm[l, b])
        p.barrier()

        SC = HD ** -0.5
        CW = (IDX_DIM ** -0.5) * (IDX_HEADS ** -0.5)

        def attention_phase(ph, NK, NKT, key_tiles_fill, q_tiles, NQ, NR):
            KT = p.sb("KT", [65, 4, NK], BF16, ph)
            VA = p.sb("VA", [128, NKT, 4, 65], BF16, ph)
            kiT2 = p.sb("kiT2", [128, NK], BF16, ph)
            kmx = p.sb("kmx", [128, 4], F32, ph)
            kmaxs = p.sb("kmaxs", [128, 16], F32, ph)
            kaug = p.sb("kaug", [128, 4, 65], F32, ph)
            ki2 = p.sb("ki2", [128, 128], F32, ph)
            ksq = p.sb("ksq", [128, 256], F32, ph)
            kn = p.sb("kn", [128, 8], F32, ph)
            kvl = [p.sb(f"kvl{i}", [128, 512], F32, ph) for i in range(2)]
            kil = [p.sb(f"kil{i}", [128, 64], F32, ph) for i in range(2)]
            Irow = p.sb("Irow", [128, NK], F32, ph)
            junk = p.sb("junk", [128, NK], U8, ph)
            Rh = [p.sb(f"Rh{i}", [128, 8, 512], BF16, ph) for i in range(NR)]
            mbT = p.sb("mbT", [128, NKT, NQ], BF16, ph)
            qt = p.sb("qt", [128, D], F32, ph)
            qaug = p.sb("qaug", [128, 16, 65], F32, ph)
            QT = p.sb("QT", [65, 16, NQ], BF16, ph)
            qil = p.sb("qil", [128, 584], F32, ph)
            qiT = p.sb("qiT", [128, 4, NQ], BF16, ph)
            diagw = p.sb("diagw", [128, 8, NQ], BF16, ph)
            bs = p.sb("bs", [128, 16], F32, ph)
            halfs = p.sb("halfs", [128, NBIS], F32, ph)
            qn = p.sb("qn", [128, 16], F32, ph)
            PT = [p.sb(f"PT{i}", [128, 512], BF16, ph) for i in range(3)]
            rs = p.sb("rs", [128, 16], F32, ph)
            yatt = p.sb("yatt", [128, D], F32, ph)

            p.memset(VA[:], 1.0)
            p.memset(kaug[:], 1.0)
            p.memset(kmx[:], 0.0)

            def fill(j, col, n, ksrc, vsrc, kisrc):
                p.copy(kaug[:n, :, 0:64], ksrc.r("p (g d) -> p g d", g=4))

                def evk(b0, cntb, pv3, grp):
                    p.copy(KT[:65, :, col:col + n], pv3[:65, :4, :n], eng="act")
                transpose_to(evk, kaug[:, :, :].r("p g d -> p (g d)"), n, [65] * 4)
                p.tt(ksq[:n, :], ksrc, ksrc, ALU.mult, eng="pool")
                p.reduce(kn[:n, 0:4], ksq[:n, :].r("p (g d) -> p g d", g=4), ALU.add)
                p.tt(kmx[:n, :], kmx[:n, :], kn[:n, 0:4], ALU.max)
                p.copy(VA[:n, j, :, 0:64], vsrc.r("p (g d) -> p g d", g=4), eng="pool")
                p.copy(ki2[:n, 0:64], kisrc, eng="pool")
                p.copy(ki2[:n, 64:128], kisrc, eng="pool")

                def evi(b0, cntb, pv3, grp):
                    p.copy(kiT2[:, col:col + n], pv3[:, 0, :n], eng="act")
                transpose_to(evi, ki2, n, [128])

            key_tiles_fill(fill, kvl, kil)

            bank = PS[3]
            p.tr(bank[:4, :128], kmx[:, :], identf[:])
            p.reduce(kn[:4, 4:5], bank[:4, :128], ALU.max)
            p.act(kn[:4, 4:5], kn[:4, 4:5], AF.Sqrt)
            p.ts(kn[:4, 5:6], kn[:4, 4:5], -SC, ALU.mult)
            p.ts(ksq[:4, 0:16], sel4[:, :], kn[:4, 5:6], ALU.mult)
            bank = PS[4]
            p.mm(bank[:, 0:16], onesf[:4, :], ksq[:4, 0:16], start=True, stop=True)
            p.copy(kmaxs[:], bank[:, 0:16])

            for qd in q_tiles:
                n = qd["n"]
                ktl = qd["ktiles"]
                end = ktl[-1][1] + ktl[-1][2]
                topk = qd["topk"]
                r0 = qd["r0"]
                p.dma(qt[:n, :], PTM[r0:r0 + n, C_Q:C_Q + D])
                p.dma(qil[:n, :], PTM[r0:r0 + n, C_QI:C_QI + 584])
                p.ts(bs[:n, 8:16], qil[:n, 576:584], CW, ALU.mult)
                for h in range(8):
                    p.ts(diagw[:n, h, :n], identf[:n, :n], bs[:n, 8 + h:9 + h], ALU.mult, eng="pool" if h % 2 else "dve")

                def evq(b0, cntb, pv3, grp):
                    p.copy(qiT[:, b0:b0 + cntb, :n], pv3[:, :cntb, :n], eng="act")
                transpose_to(evq, qil, n, [128] * 4)
                for ci, c0 in enumerate(range(0, end, 512)):
                    w = min(512, end - c0)
                    R = Rh[ci % NR]
                    for h in range(8):
                        xb = PS[h % 3]
                        po = 64 * (h % 2)
                        p.mm(xb[:n, :w], qiT[po:po + 64, h // 2, :n], kiT2[po:po + 64, c0:c0 + w], start=True, stop=True)
                        p.act(R[:n, h, :w], xb[:n, :w], AF.Relu)
                    ib = PS[3 + ci % 2]
                    for h in range(8):
                        p.mm(ib[:n, :w], diagw[:n, h, :n], R[:n, h, :w], start=(h == 0), stop=(h == 7))
                    p.copy(Irow[:n, c0:c0 + w], ib[:n, :w], eng="dve")
                p.reduce(bs[:n, 0:1], Irow[:n, :end], ALU.max, absval=True)
                dt_i, dcol, dn = ktl[-1]
                p.tt(Irow[:n, dcol:dcol + dn], Irow[:n, dcol:dcol + dn], cmask[:n, :dn], ALU.add)
                p.ts(bs[:n, 1:2], bs[:n, 0:1], -1.0001, ALU.mult, -1e-6, ALU.add)
                if end > topk:
                    p.ts(bs[:n, 2:3], bs[:n, 1:2], -2.0, ALU.mult)
                    p.tt(halfs[:n, :], pow2[:n, :], bs[:n, 2:3].bc([n, NBIS]), ALU.mult)
                    for k in range(NBIS):
                        p.tt(bs[:n, 3:4], bs[:n, 1:2], halfs[:n, k:k + 1], ALU.add)
                        p.ts(junk[:n, :end], Irow[:n, :end], bs[:n, 3:4], ALU.is_ge, None, ALU.add, accum=bs[:n, 4:5])
                        p.stt(bs[:n, 5:6], bs[:n, 4:5], topk - 0.5, halfs[:n, k:k + 1], ALU.is_ge, ALU.mult)
                        p.tt(bs[:n, 1:2], bs[:n, 1:2], bs[:n, 5:6], ALU.add)
                p.ts(Irow[:n, :end], Irow[:n, :end], bs[:n, 1:2], ALU.is_ge, -1.0, ALU.add)
                for b0 in range(0, len(ktl), 4):
                    grp = ktl[b0:b0 + 4]
                    bank = PS[3 + (rr["ps"] % 2)]
                    rr["ps"] += 1
                    pv3 = bank[:].r("p (a b) -> p a b", a=4)
                    for gi, (tj, col, nk) in enumerate(grp):
                        p.tr(pv3[:nk, gi, :n], Irow[:n, col:col + nk], identf[:n, :n])
                    if all(g[2] == 128 for g in grp) and grp[-1][0] - grp[0][0] == len(grp) - 1:
                        p.ts(mbT[:, grp[0][0]:grp[0][0] + len(grp), :n], pv3[:, :len(grp), :n], -NEG, ALU.mult, eng="dve")
                    else:
                        for gi, (tj, col, nk) in enumerate(grp):
                            p.ts(mbT[:nk, tj, :n], pv3[:nk, gi, :n], -NEG, ALU.mult, eng="dve")
                p.tt(yatt[:n, :], qt[:n, :], qt[:n, :], ALU.mult, eng="pool")
                p.reduce(qn[:n, :], yatt[:n, :].r("p (h d) -> p h d", h=16), ALU.add)
                p.act(qn[:n, :], qn[:n, :], AF.Sqrt)
                p.tt(qaug[:n, :, 64:65].r("p h o -> p (h o)"), qn[:n, :], kmaxs[:n, :], ALU.mult)
                p.ts(qaug[:n, :, 0:64], qt[:n, :].r("p (h d) -> p h d", h=16), SC, ALU.mult)

                def evQ(b0, cntb, pv3, grp):
                    p.copy(QT[:65, b0:b0 + cntb, :n], pv3[:65, :cntb, :n], eng="act")
                transpose_to(evQ, qaug[:, :, :].r("p h d -> p (h d)"), n, [65] * 16)
                for bk in range(3):
                    p.mm(PS[5 + bk][:n, 0:390], zerob[:1, :n], zerob[:1, 0:390], start=True, stop=False, skip=True)
                cnt = 0
                for ji, (tj, col, nk) in enumerate(ktl):
                    last = ji == len(ktl) - 1
                    for g in range(4):
                        sb_ = PS[cnt % 3]
                        pt = PT[cnt % 3]
                        cnt += 1
                        p.mm(sb_[:nk, :4 * n].r("p (a b) -> p a b", a=4), KT[:65, g, col:col + nk], QT[:65, 4 * g:4 * g + 4, :n], start=True, stop=False)
                        p.mm(sb_[:nk, :4 * n].r("p (a b) -> p a b", a=4), identb[:nk, :nk], mbT[:nk, tj:tj + 1, :n].bc([nk, 4, n]), start=False, stop=True)
                        p.act(pt[:nk, :4 * n], sb_[:nk, :4 * n], AF.Exp)
                        for hh in range(4):
                            h = 4 * g + hh
                            p.mm(PS[5 + h // 6][:n, (h % 6) * 65:(h % 6) * 65 + 65], pt[:nk, hh * n:(hh + 1) * n], VA[:nk, tj, g, :], start=False, stop=last, skip=True)
                for bk in range(3):
                    nh = 6 if bk < 2 else 4
                    ov = PS[5 + bk][:n, 0:nh * 65].r("p (h d) -> p h d", h=nh)
                    p.recip(rs[:n, bk * 6:bk * 6 + nh].us(2), ov[:, :, 64:65])
                    p.tt(yatt[:n, bk * 384:bk * 384 + nh * 64].r("p (h d) -> p h d", h=nh), ov[:, :, 0:64], rs[:n, bk * 6:bk * 6 + nh].us(2).bc([n, nh, 64]), ALU.mult)
                qd["sink"](yatt, n, qd)
            return yatt

        for b in range(cfg.nbs):
            with ExitStack() as ph:
                NKs = cfg.past + cfg.ds
                NKTs = cfg.n_pages + 1
                ptb = p.sb("ptb", [128, cfg.n_pages], I32, ph)
                ptf = p.sb("ptf", [128, cfg.n_pages], F32, ph)
                iot = p.sb("iot", [128, 1], F32, ph)
                idx = p.sb("idx", [128, cfg.n_pages], U32, ph)
                bcast_row(ptb[:], ptab[b])
                p.op("pool", lambda e: e.iota(iot.t[:], pattern=[[0, 1]], base=0, channel_multiplier=1, allow_small_or_imprecise_dtypes=True), [], [iot])
                p.copy(ptf[:], ptb[:])
                p.ts(ptf[:], ptf[:], float(PAGE), ALU.mult, iot[:, 0:1], ALU.add)
                if l > 0:
                    p.ts(ptf[:], ptf[:], float(l * cfg.n_pool * PAGE), ALU.add)
                p.copy(idx[:], ptf[:])

                def fill_s(fill, kvl, kil, b=b, idx=idx):
                    for j in range(cfg.n_pages):
                        kv = kvl[j % 2]
                        ki = kil[j % 2]
                        for (dstv, pool_t) in ((kv[:, 0:256], cache_k), (kv[:, 256:512], cache_v), (ki[:, :], cache_ki)):
                            p.dma(dstv, pool_t[:], q="pool", fn=lambda e, dstv=dstv, pool_t=pool_t, j=j: e.indirect_dma_start(
                                out=dstv.ap, out_offset=None, in_=pool_t.t[:], in_offset=bass.IndirectOffsetOnAxis(ap=idx.t[:, j:j + 1], axis=0)),
                                  extra_reads=[idx])
                        fill(j, j * PAGE, PAGE, kv[:, 0:256], kv[:, 256:512], ki[:, :])
                    r = TP + b * cfg.ds
                    kv = kvl[cfg.n_pages % 2]
                    ki = kil[cfg.n_pages % 2]
                    p.dma(kv[:cfg.ds, :], PTM[r:r + cfg.ds, C_K:C_K + 512])
                    p.dma(ki[:cfg.ds, :], PTM[r:r + cfg.ds, C_KI:C_KI + 64])
                    fill(cfg.n_pages, cfg.past, cfg.ds, kv[:cfg.ds, 0:256], kv[:cfg.ds, 256:512], ki[:cfg.ds, :])

                def sink_s(yatt, n, qd, b=b):
                    p.dma(YAS[b * cfg.ds:(b + 1) * cfg.ds, :], yatt[:n, :], q="pool")

                ktl = [(j, j * PAGE, PAGE) for j in range(cfg.n_pages)] + [(cfg.n_pages, cfg.past, cfg.ds)]
                attention_phase(ph, NKs, NKTs, fill_s, [dict(n=cfg.ds, ktiles=ktl, topk=cfg.topk_s, r0=TP + b * cfg.ds, sink=sink_s)], cfg.ds, 1)
            p.barrier()

        with ExitStack() as ph:
            wat = p.sb("wat", [128, 8, D], BF16, ph)
            wou = p.sb("wou", [128, 8, D], BF16, ph)
            stg = [p.sb(f"stgb{i}", [128, 1024], F32, ph) for i in range(2)]
            load_w_bf16(wat, lambda kc, n0, w: w_attn[l, kc * 128:(kc + 1) * 128, n0:n0 + w], 8, D, stg)
            load_w_bf16(wou, lambda kc, n0, w: w_out[l, kc * 128:(kc + 1) * 128, n0:n0 + w], 8, D, stg)
            n2b = p.sb("n2b", [128, D], F32, ph)
            bcast_row(n2b[:], norm2_w[l])
            ga = p.sb("ga", [128, D], F32, ph)
            msl = p.sb("msl", [128, D], F32, ph)
            hl = p.sb("hl", [128, D], F32, ph)
            aT = p.sb("aT", [128, 8, 128], BF16, ph)
            ssq2 = p.sb("ssq2", [128, 2], F32, ph)
            u2Tt = p.sb("u2Tt", [128, 8, 128], BF16, ph)

            def post_chain(yatt, n, qd):
                r0 = qd["r0"]
                p.dma(ga[:n, :], PTM[r0:r0 + n, C_GA:C_GA + D])
                p.dma(msl[:n, :], MS[r0:r0 + n, :])
                for (src, po, cnt) in h_rows(l, r0, n):
                    p.dma(hl[po:po + cnt, :], src)

                def ev(b0, cntb, pv3, grp):
                    p.copy(aT[:, b0:b0 + cntb, :n], pv3[:, :cntb, :n], eng="act")
                transpose_to(ev, yatt, n, [128] * 8)
                p.act(ga[:n, :], ga[:n, :], AF.Sigmoid)
                for c2 in range(2):
                    bank = PS[c2]
                    for kc in range(8):
                        p.mm(bank[:n, :], aT[:, kc, :n], wat[:, kc, c2 * 512:(c2 + 1) * 512], start=(kc == 0), stop=(kc == 7))
                    p.tt(ga[:n, c2 * 512:(c2 + 1) * 512], bank[:n, :], ga[:n, c2 * 512:(c2 + 1) * 512], ALU.mult)
                p.tt(msl[:n, :], msl[:n, :], ga[:n, :], ALU.add)
                transpose_to(ev, msl, n, [128] * 8)
                for c2 in range(2):
                    bank = PS[c2]
                    for kc in range(8):
                        p.mm(bank[:n, :], aT[:, kc, :n], wou[:, kc, c2 * 512:(c2 + 1) * 512], start=(kc == 0), stop=(kc == 7))
                    p.tt(hl[:n, c2 * 512:(c2 + 1) * 512], bank[:n, :], hl[:n, c2 * 512:(c2 + 1) * 512], ALU.add)
                p.dma(HM[r0:r0 + n, :], hl[:n, :], q="pool")
                rmsnorm_tile(hl, n, n2b, msl, ssq2, ga)

                def ev2(b0, cntb, pv3, grp):
                    p.copy(u2Tt[:, b0:b0 + cntb, :n], pv3[:, :cntb, :n], eng="act")
                transpose_to(ev2, msl, n, [128] * 8)
                p.dma(U2T[:, :, r0:r0 + n].r("kc p t -> p kc t"), u2Tt[:, :, :n], q="pool", slow=True)

            def fill_p(fill, kvl, kil):
                for j, (r0, n) in enumerate(cfg.tiles):
                    kv = kvl[j % 2]
                    ki = kil[j % 2]
                    p.dma(kv[:n, :], PTM[r0:r0 + n, C_K:C_K + 512])
                    p.dma(ki[:n, :], PTM[r0:r0 + n, C_KI:C_KI + 64])
                    fill(j, r0, n, kv[:n, 0:256], kv[:n, 256:512], ki[:n, :])

            q_tiles = []
            for i, (r0, n) in enumerate(cfg.tiles):
                ktl = [(j, c, m) for j, (c, m) in enumerate(cfg.tiles[:i + 1])]
                q_tiles.append(dict(n=n, ktiles=ktl, topk=cfg.topk_p, r0=r0, sink=post_chain))
            ys_t = attention_phase(ph, TP, NT, fill_p, q_tiles, 128, 2)
            p.dma(ys_t[:NS, :], YAS[:, :])
            post_chain(ys_t, NS, dict(r0=TP))
        p.barrier()

        with ExitStack() as ph:
            wup = p.sb("wup", [128, 8, D_FF], BF16, ph)
            wdn = p.sb("wdn", [128, 32, D], BF16, ph)
            stg = [p.sb(f"stgc{i}", [128, 1024], F32, ph) for i in range(2)]
            load_w_bf16(wup, lambda kc, n0, w: w_up[l, kc * 128:(kc + 1) * 128, n0:n0 + w], 8, D_FF, stg)
            load_w_bf16(wdn, lambda kc, n0, w: w_down[l, kc * 128:(kc + 1) * 128, n0:n0 + w], 32, D, stg)
            GW = 256
            u2g = [p.sb(f"u2g{i}", [128, 8, GW], BF16, ph) for i in range(2)]
            aTm = p.sb("aTm", [128, 32, GW], BF16, ph)
            rl = [p.sb(f"rl{i}", [128, GW], F32, ph) for i in range(2)]
            hm = [p.sb(f"hm{i}", [128, D], F32, ph) for i in range(2)]
            ho = [p.sb(f"ho{i}", [128, D], F32, ph) for i in range(2)]
            last = l == L - 1
            if last:
                fnb = p.sb("fnb", [128, D], F32, ph)
                bcast_row(fnb[:], fnorm_w[0])
                sqf = p.sb("sqf", [128, D], F32, ph)
                ssqf = p.sb("ssqf", [128, 2], F32, ph)
            gi = 0
            bi = 0
            for g0 in range(0, TT, GW):
                gw = min(GW, TT - g0)
                ug = u2g[gi % 2]
                gi += 1
                p.dma(ug[:, :, :gw], U2T[:, :, g0:g0 + gw].r("kc p t -> p kc t"), slow=True)
                for fc in range(32):
                    bank = PS[fc % 3]
                    for kc in range(8):
                        p.mm(bank[:, :gw], wup[:, kc, fc * 128:(fc + 1) * 128], ug[:, kc, :gw], start=(kc == 0), stop=(kc == 7))
                    r = rl[fc % 2]
                    p.act(r[:, :gw], bank[:, :gw], AF.Relu)
                    p.tt(aTm[:, fc, :gw], r[:, :gw], r[:, :gw], ALU.mult, eng="dve" if fc % 2 else "pool")
                for s0 in range(0, gw, 128):
                    n = min(128, gw - s0)
                    r0 = g0 + s0
                    hmt = hm[bi % 2]
                    hot = ho[bi % 2]
                    bi += 1
                    p.dma(hmt[:n, :], HM[r0:r0 + n, :])
                    for c2 in range(2):
                        bank = PS[3 + c2]
                        for fc in range(32):
                            p.mm(bank[:n, :], aTm[:, fc, s0:s0 + n], wdn[:, fc, c2 * 512:(c2 + 1) * 512], start=(fc == 0), stop=(fc == 31))
                        p.tt(hot[:n, c2 * 512:(c2 + 1) * 512], bank[:n, :], hmt[:n, c2 * 512:(c2 + 1) * 512], ALU.add)
                    if not last:
                        p.dma(H[l + 1][r0:r0 + n, :], hot[:n, :], q="pool")
                    else:
                        rmsnorm_tile(hot, n, fnb, hmt, ssqf, sqf)
                        for (a, bnd, dst, off) in ((N_META, TP, y_p, N_META), (TP, TT, y_s, TP)):
                            lo, hi = max(a, r0), min(bnd, r0 + n)
                            if lo < hi:
                                p.dma(dst[lo - off:hi - off, :], hmt[lo - r0:hi - r0, :], q="pool")
        p.barrier()

    p.barrier()
    return nc, es, p


_CACHE = {}


def _run(cfg, per_core_inputs):
    key = (cfg.seq, cfg.nbs, cfg.n_pages, cfg.n_pool, cfg.depth)
    nc, es, p = build(cfg)
    res = run_bass_kernel_spmd(nc, per_core_inputs, core_ids=list(range(len(per_core_inputs))))
    return res.results


def kernel(x_prompt, x_sample, cache_k, cache_v, cache_kidx, state_ssm, state_conv, page_table,
           meta_tokens, norm1_w, w_in, conv_w, conv_b, dt_bias, a_log, d_skip, ssm_norm_w,
           w_ssm_proj, w_attn_proj, w_out, norm2_w, w_up, w_down, final_norm_w):
    f = lambda a: np.ascontiguousarray(np.asarray(a, dtype=np.float32))
    x_prompt, x_sample = f(x_prompt), f(x_sample)
    B, SEQ, _ = x_prompt.shape
    DB, DS, _ = x_sample.shape
    L = w_in.shape[0]
    n_pool = cache_k.shape[1]
    n_pages = page_table.shape[1]
    NCORE = 8
    nbs = DB // NCORE
    cfg = Cfg(seq=SEQ, nbs=nbs, dec_seq=DS, n_pages=n_pages, n_pool=n_pool, depth=L)
    ck = f(cache_k).reshape(L * n_pool * PAGE, KVD)
    cv = f(cache_v).reshape(L * n_pool * PAGE, KVD)
    cki = f(cache_kidx).reshape(L * n_pool * PAGE, IDX_DIM)
    shared = {
        "meta_tokens": f(meta_tokens), "cache_k": ck, "cache_v": cv, "cache_kidx": cki,
        "norm1_w": f(norm1_w), "w_in": f(w_in), "conv_w": f(conv_w), "conv_b": f(conv_b), "dt_bias": f(dt_bias),
        "a_log": f(a_log), "d_skip": f(d_skip), "ssm_norm_w": f(ssm_norm_w), "w_ssm_proj": f(w_ssm_proj),
        "w_attn_proj": f(w_attn_proj), "w_out": f(w_out), "norm2_w": f(norm2_w), "w_up": f(w_up), "w_down": f(w_down),
        "final_norm_w": f(final_norm_w).reshape(1, D),
    }
    ssm = f(state_ssm)
    scv = f(state_conv)
    pt = np.ascontiguousarray(np.asarray(page_table, dtype=np.int32))
    in_maps = []
    for c in range(NCORE):
        m = dict(shared)
        m["xp"] = x_prompt[c % B]
        m["xs"] = np.ascontiguousarray(x_sample[c * nbs:(c + 1) * nbs].reshape(nbs * DS, D))
        m["state_ssm"] = np.ascontiguousarray(ssm[:, c * nbs:(c + 1) * nbs].reshape(L, nbs, D_INNER, NSTATE))
        m["state_conv"] = np.ascontiguousarray(scv[:, c * nbs:(c + 1) * nbs])
        m["page_table"] = np.ascontiguousarray(pt[c * nbs:(c + 1) * nbs])
        in_maps.append(m)
    res = _run(cfg, in_maps)
    TP = cfg.tp
    y_prompt = np.stack([res[b]["y_p"] for b in range(B)])
    y_sample = np.concatenate([res[c]["y_s"].reshape(nbs, DS, D) for c in range(NCORE)])
    pk = np.stack([res[b]["pk"].reshape(L, TP, 4, HD) for b in range(B)], axis=1)
    pv = np.stack([res[b]["pv"].reshape(L, TP, 4, HD) for b in range(B)], axis=1)
    pki = np.stack([res[b]["pki"] for b in range(B)], axis=1)
    pssm = np.stack([res[b]["pssm"].reshape(L, SSM_HEADS, HD, NSTATE) for b in range(B)], axis=1)
    pconv = np.stack([res[b]["pconv"] for b in range(B)], axis=1)
    sk = np.concatenate([res[c]["sk"].reshape(L, nbs, DS, 4, HD) for c in range(NCORE)], axis=1)
    sv = np.concatenate([res[c]["sv"].reshape(L, nbs, DS, 4, HD) for c in range(NCORE)], axis=1)
    ski = np.concatenate([res[c]["ski"].reshape(L, nbs, DS, IDX_DIM) for c in range(NCORE)], axis=1)
    sssm = np.concatenate([res[c]["sssm"].reshape(L, nbs, SSM_HEADS, HD, NSTATE) for c in range(NCORE)], axis=1)
    sconv = np.concatenate([res[c]["sconv"] for c in range(NCORE)], axis=1)
    return (y_prompt, y_sample, pk, pv, pki, pssm, pconv, sk, sv, ski, sssm, sconv)
```

```python
import math
from contextlib import ExitStack
import numpy as np
import concourse.bass as bass
import concourse.mybir as mybir
from concourse.bass_utils import run_bass_kernel_spmd

F32 = mybir.dt.float32
BF16 = mybir.dt.bfloat16
I32 = mybir.dt.int32
U32 = mybir.dt.uint32
U8 = mybir.dt.uint8
AF = mybir.ActivationFunctionType
ALU = mybir.AluOpType
AX = mybir.AxisListType

D = 1024
N_META = 16
EPS = 1e-6
D_INNER = 2048
SSM_HEADS = 32
HD = 64
NSTATE = 128
CONV_DIM = 3072
KVD = 256
IDX_HEADS = 8
IDX_DIM = 64
D_FF = 4096
IN_DIM = 9320
PAGE = 128
C_Z, C_XBC, C_DT, C_Q, C_K, C_V, C_QI, C_KI, C_WI, C_GS, C_GA = 0, 2048, 5120, 5152, 6176, 6432, 6688, 7200, 7264, 7272, 8296
NEG = -30000.0
NBIS = 16


class Cfg:
    def __init__(self, seq=4096, nbs=4, dec_seq=8, n_pages=64, n_pool=2560, depth=2):
        self.seq = seq
        self.nbs = nbs
        self.ds = dec_seq
        self.n_pages = n_pages
        self.n_pool = n_pool
        self.depth = depth
        self.tp = N_META + seq
        self.ns = nbs * dec_seq
        self.tt = self.tp + self.ns
        self.topk_p = min(256, seq // 4)
        self.topk_s = min(256, (n_pages * PAGE + dec_seq) // 4)
        self.tiles = [(0, N_META)] + [(N_META + 128 * k, 128) for k in range(seq // 128)]
        self.past = n_pages * PAGE


class V:
    __slots__ = ("b", "ap")

    def __init__(self, b, ap):
        self.b = b
        self.ap = ap

    def bc(self, shape):
        return V(self.b, self.ap.to_broadcast(list(shape)))

    def us(self, axis):
        return V(self.b, self.ap.unsqueeze(axis))

    def r(self, pat, **kw):
        return V(self.b, self.ap.rearrange(pat, **kw))

    def __getitem__(self, k):
        return V(self.b, self.ap[k])


class Buf:
    __slots__ = ("t", "w", "r", "name", "free")

    def __init__(self, t, name="", free=False):
        self.t = t
        self.w = None
        self.r = []
        self.name = name
        self.free = free

    def __getitem__(self, k):
        return V(self, self.t[k])


class Prog:
    ENG = ("pe", "act", "dve", "pool", "sp")

    def __init__(self, nc, es, ring=20):
        self.nc = nc
        self.es = es
        self.e = {"pe": nc.tensor, "act": nc.scalar, "dve": nc.vector, "pool": nc.gpsimd, "sp": nc.sync}
        self.sem = {k: es.enter_context(nc.semaphore("s_" + k)) for k in self.ENG}
        self.epoch = {k: 0 for k in self.ENG}
        self.cnt = {k: 0 for k in self.ENG}
        self.total = {k: 0 for k in self.ENG}
        self.seen = {k: {} for k in self.ENG}
        self.ring = {q: [[es.enter_context(nc.semaphore(f"d_{q}{i}")), 0] for i in range(ring)] for q in ("sp", "pool")}
        self.rpos = {q: 0 for q in self.ring}
        self.ninst = 0
        self.cast_rr = 0

    def sb(self, name, shape, dt, es=None):
        self.uid = getattr(self, "uid", 0) + 1
        name = f"{name}_{self.uid}"
        return Buf((es or self.es).enter_context(self.nc.sbuf_tensor(name, list(shape), dt)), name)

    def ps(self, name, shape, dt=F32):
        return Buf(self.es.enter_context(self.nc.psum_tensor(name, list(shape), dt)), name)

    def dram(self, name, shape, dt, kind="Internal"):
        return Buf(self.nc.dram_tensor(name, list(shape), dt, kind=kind), name, free=True)

    def _wait(self, eng, tok):
        if tok is None:
            return
        sem, val, key = tok
        if self.seen[eng].get(key, 0) >= val:
            return
        self.e[eng].wait_ge(sem, val)
        self.seen[eng][key] = val

    def _deps(self, eng, reads, writes, acc):
        for b in reads:
            if not b.free:
                self._wait(eng, b.w)
        for b in writes:
            if b.free:
                continue
            if not (acc and b.w is not None and b.w[2].split("#")[0] == eng):
                self._wait(eng, b.w)
            for t in b.r:
                self._wait(eng, t)

    def _commit(self, tok, reads, writes):
        for b in reads:
            if not b.free:
                b.r = [t for t in b.r if t[2] != tok[2]] + [tok]
        for b in writes:
            if not b.free:
                b.w = tok
                b.r = []

    def op(self, eng, fn, reads=(), writes=(), acc=False):
        reads = [v.b if isinstance(v, V) else v for v in reads]
        writes = [v.b if isinstance(v, V) else v for v in writes]
        self._deps(eng, reads, writes, acc)
        ins = fn(self.e[eng])
        if self.cnt[eng] >= 30000:
            self.epoch[eng] += 1
            self.sem[eng] = self.es.enter_context(self.nc.semaphore(f"s_{eng}_{self.epoch[eng]}"))
            self.cnt[eng] = 0
        self.cnt[eng] += 1
        self.total[eng] += 1
        ins.then_inc(self.sem[eng], 1)
        tok = (self.sem[eng], self.cnt[eng], f"{eng}#{self.epoch[eng]}")
        self._commit(tok, reads, writes)
        self.ninst += 1
        return ins

    def dma(self, out, in_, q="sp", slow=False, fn=None, extra_reads=()):
        ring = self.ring[q]
        i = self.rpos[q]
        self.rpos[q] = (i + 1) % len(ring)
        slot = ring[i]
        key = f"d_{q}{i}"
        if slot[1] > 0:
            self._wait(q, (slot[0], slot[1], key))
        reads, writes = [in_.b] + [v.b if isinstance(v, V) else v for v in extra_reads], [out.b]
        self._deps(q, reads, writes, False)
        if fn is not None:
            ins = fn(self.e[q])
        elif slow:
            ins = self.e[q].dma_start(out=out.ap, in_=in_.ap, allow_slow_non_contiguous=True)
        else:
            ins = self.e[q].dma_start(out=out.ap, in_=in_.ap)
        slot[1] += 16
        ins.then_inc(slot[0], 16)
        tok = (slot[0], slot[1], key)
        self._commit(tok, reads, writes)
        self.ninst += 1
        return ins

    def barrier(self, bufs=()):
        toks = [(self.sem[k], self.cnt[k], f"{k}#{self.epoch[k]}") for k in self.ENG if self.cnt[k] > 0]
        for q, ring in self.ring.items():
            for i, slot in enumerate(ring):
                if slot[1] > 0:
                    toks.append((slot[0], slot[1], f"d_{q}{i}"))
        for eng in self.ENG:
            for t in toks:
                if t[2].split("#")[0] != eng:
                    self._wait(eng, t)

    def act(self, out, in_, func, bias=None, scale=None, accum=None, eng="act"):
        kw = {}
        reads = [in_]
        writes = [out]
        if bias is not None:
            if isinstance(bias, V):
                kw["bias"] = bias.ap
                reads.append(bias)
            else:
                kw["bias"] = bias
        if scale is not None:
            if isinstance(scale, V):
                kw["scale"] = scale.ap
                reads.append(scale)
            else:
                kw["scale"] = scale
        if accum is not None:
            kw["accum_out"] = accum.ap
            writes.append(accum)
        return self.op("act", lambda e: e.activation(out=out.ap, in_=in_.ap, func=func, **kw), reads, writes)

    def tt(self, out, in0, in1, op, eng="dve"):
        return self.op(eng, lambda e: e.tensor_tensor(out=out.ap, in0=in0.ap, in1=in1.ap, op=op), [in0, in1], [out])

    def ts(self, out, in0, s1, op0, s2=None, op1=None, accum=None, eng="dve"):
        reads = [in0]
        writes = [out]
        a1 = s1.ap if isinstance(s1, V) else s1
        a2 = s2.ap if isinstance(s2, V) else s2
        if isinstance(s1, V):
            reads.append(s1)
        if isinstance(s2, V):
            reads.append(s2)
        kw = {}
        if op1 is not None:
            kw["op1"] = op1
        if accum is not None:
            kw["accum_out"] = accum.ap
            writes.append(accum)
        return self.op(eng, lambda e: e.tensor_scalar(out=out.ap, in0=in0.ap, scalar1=a1, scalar2=a2, op0=op0, **kw), reads, writes)

    def stt(self, out, in0, s, in1, op0, op1):
        reads = [in0, in1]
        a = s.ap if isinstance(s, V) else s
        if isinstance(s, V):
            reads.append(s)
        return self.op("dve", lambda e: e.scalar_tensor_tensor(out=out.ap, in0=in0.ap, scalar=a, in1=in1.ap, op0=op0, op1=op1), reads, [out])

    def copy(self, out, in_, eng="dve"):
        if eng == "act":
            return self.act(out, in_, AF.Copy)
        return self.op(eng, lambda e: e.tensor_copy(out=out.ap, in_=in_.ap), [in_], [out])

    def memset(self, out, val, eng="pool"):
        return self.op(eng, lambda e: e.memset(out.ap, val), [], [out])

    def reduce(self, out, in_, op, axis=AX.X, absval=False):
        kw = {"apply_absolute_value": True} if absval else {}
        return self.op("dve", lambda e: e.tensor_reduce(out=out.ap, in_=in_.ap, axis=axis, op=op, **kw), [in_], [out])

    def recip(self, out, in_):
        return self.op("dve", lambda e: e.reciprocal(out=out.ap, in_=in_.ap), [in_], [out])

    def mm(self, out, lhsT, rhs, start=True, stop=True, skip=False):
        return self.op("pe", lambda e: e.matmul(out.ap, lhsT=lhsT.ap, rhs=rhs.ap, start=start, stop=stop, skip_group_check=skip),
                       [lhsT, rhs], [out], acc=True)

    def tr(self, out, in_, ident):
        return self.op("pe", lambda e: e.transpose(out=out.ap, in_=in_.ap, identity=ident.ap), [in_, ident], [out], acc=True)


def build(cfg):
    nc = bass.Bass("TRN2", target_bir_lowering=False)
    es = ExitStack()
    p = Prog(nc, es)
    TT, TP, NS = cfg.tt, cfg.tp, cfg.ns
    L = cfg.depth
    NT = len(cfg.tiles)
    def din(name, shape, dt=F32):
        return p.dram(name, shape, dt, kind="ExternalInput")

    def dout(name, shape):
        return p.dram(name, shape, F32, kind="ExternalOutput")

    xp = din("xp", [cfg.seq, D])
    xs = din("xs", [NS, D])
    meta = din("meta_tokens", [N_META, D])
    cache_k = din("cache_k", [L * cfg.n_pool * PAGE, KVD])
    cache_v = din("cache_v", [L * cfg.n_pool * PAGE, KVD])
    cache_ki = din("cache_kidx", [L * cfg.n_pool * PAGE, IDX_DIM])
    st_ssm = din("state_ssm", [L, cfg.nbs, D_INNER, NSTATE])
    st_conv = din("state_conv", [L, cfg.nbs, 3, CONV_DIM])
    ptab = din("page_table", [cfg.nbs, cfg.n_pages], I32)
    norm1_w = din("norm1_w", [L, D])
    w_in = din("w_in", [L, D, IN_DIM])
    conv_w = din("conv_w", [L, 4, CONV_DIM])
    conv_b = din("conv_b", [L, CONV_DIM])
    dt_bias = din("dt_bias", [L, SSM_HEADS])
    a_log = din("a_log", [L, SSM_HEADS])
    d_skip = din("d_skip", [L, SSM_HEADS])
    ssm_norm_w = din("ssm_norm_w", [L, D_INNER])
    w_ssm = din("w_ssm_proj", [L, D_INNER, D])
    w_attn = din("w_attn_proj", [L, D, D])
    w_out = din("w_out", [L, D, D])
    norm2_w = din("norm2_w", [L, D])
    w_up = din("w_up", [L, D, D_FF])
    w_down = din("w_down", [L, D_FF, D])
    fnorm_w = din("final_norm_w", [1, D])

    y_p = dout("y_p", [cfg.seq, D])
    y_s = dout("y_s", [NS, D])
    pk = dout("pk", [L, TP, KVD])
    pv = dout("pv", [L, TP, KVD])
    pki = dout("pki", [L, TP, IDX_DIM])
    pssm = dout("pssm", [L, D_INNER, NSTATE])
    pconv = dout("pconv", [L, 3, CONV_DIM])
    sk = dout("sk", [L, NS, KVD])
    sv = dout("sv", [L, NS, KVD])
    ski = dout("ski", [L, NS, IDX_DIM])
    sssm = dout("sssm", [L, cfg.nbs, D_INNER, NSTATE])
    sconv = dout("sconv", [L, cfg.nbs, 3, CONV_DIM])
    outs = [y_p, y_s, pk, pv, pki, pssm, pconv, sk, sv, ski, sssm, sconv]

    H = [None] + [p.dram(f"H{l}", [TT, D], F32) for l in range(1, L)]
    HM = p.dram("HM", [TT, D], F32)
    PTM = p.dram("PTM", [TT, IN_DIM], F32)
    XBT = p.dram("XBT", [CONV_DIM, TT], BF16)
    MS = p.dram("MS", [TT, D], F32)
    U2T = p.dram("U2T", [8, 128, TT], BF16)
    YAS = p.dram("YAS", [NS, D], F32)

    identf = p.sb("identf", [128, 128], F32)
    identb = p.sb("identb", [128, 128], BF16)
    tri = p.sb("tri", [128, 128], F32)
    cmask = p.sb("cmask", [128, 128], F32)
    lstr = p.sb("lstr", [128, 128], F32)
    onesf = p.sb("onesf", [128, 128], F32)
    onesb = p.sb("onesb", [1, 512], BF16)
    zerob = p.sb("zerob", [1, 512], BF16)
    pow2 = p.sb("pow2", [128, NBIS + 1], F32)
    sel4 = p.sb("sel4", [4, 16], F32)
    PS = [p.ps(f"ps{i}", [128, 512], F32) for i in range(8)]

    p.memset(identf[:], 1.0)
    p.op("pool", lambda e: e.affine_select(out=identf.t[:], in_=identf.t[:], pattern=[[-1, 128]], compare_op=ALU.is_equal, fill=0.0, base=0, channel_multiplier=1), [identf], [identf])
    p.copy(identb[:], identf[:], eng="pool")
    p.memset(tri[:], 1.0)
    p.op("pool", lambda e: e.affine_select(out=tri.t[:], in_=tri.t[:], pattern=[[1, 128]], compare_op=ALU.is_ge, fill=0.0, base=0, channel_multiplier=-1), [tri], [tri])
    p.memset(cmask[:], 0.0)
    p.op("pool", lambda e: e.affine_select(out=cmask.t[:], in_=cmask.t[:], pattern=[[-1, 128]], compare_op=ALU.is_ge, fill=-1e30, base=0, channel_multiplier=1), [cmask], [cmask])
    p.memset(onesf[:], 1.0)
    p.memset(lstr[:], 1.0)
    p.op("pool", lambda e: e.affine_select(out=lstr.t[:], in_=lstr.t[:], pattern=[[-1, 128]], compare_op=ALU.is_gt, fill=0.0, base=0, channel_multiplier=1), [lstr], [lstr])
    p.memset(onesb[:], 1.0)
    p.memset(zerob[:], 0.0)
    for k in range(NBIS + 1):
        p.memset(pow2[:, k:k + 1], 2.0 ** -(k + 1))
    p.copy(sel4[:].r("p (a b) -> p a b", a=4), identf[:4, :4].us(2).bc([4, 4, 4]), eng="pool")

    rr = {"ps": 0}

    def transpose_to(dst_fn, src, n_rows, widths, evac_eng="act", scale=None, banks=(3, 4)):
        col = 0
        blocks = []
        for w in widths:
            blocks.append((col, w))
            col += w
        for b0 in range(0, len(blocks), 4):
            grp = blocks[b0:b0 + 4]
            bank = PS[banks[rr["ps"] % len(banks)]]
            rr["ps"] += 1
            pv3 = bank[:].r("p (a b) -> p a b", a=4)
            for gi, (c0, w) in enumerate(grp):
                p.tr(pv3[:w, gi, :n_rows], src[:n_rows, c0:c0 + w], identf[:n_rows, :n_rows])
            dst_fn(b0, len(grp), pv3, grp)

    def load_w_bf16(dst, src_ap_fn, KC, N, stage_bufs):
        i = 0
        for kc in range(KC):
            for n0 in range(0, N, 1024):
                w = min(1024, N - n0)
                st = stage_bufs[i % len(stage_bufs)]
                i += 1
                p.dma(st[:, :w], src_ap_fn(kc, n0, w))
                eng = ("pool", "act", "dve")[p.cast_rr % 3]
                p.cast_rr += 1
                p.copy(dst[:, kc, n0:n0 + w], st[:, :w], eng=eng)

    def bcast_row(dst, src_row_v, q="sp"):
        p.dma(dst, V(src_row_v.b, src_row_v.ap.partition_broadcast(128)), q=q)

    def rmsnorm_tile(x, n, wb, out, ssq, tmp):
        p.act(tmp[:n, :], x[:n, :], AF.Square, accum=ssq[:n, 0:1])
        p.ts(ssq[:n, 1:2], ssq[:n, 0:1], 1.0 / D, ALU.mult, EPS, ALU.add)
        p.act(ssq[:n, 1:2], ssq[:n, 1:2], AF.Sqrt)
        p.recip(ssq[:n, 1:2], ssq[:n, 1:2])
        p.stt(out[:n, :], x[:n, :], ssq[:n, 1:2], wb[:n, :], ALU.mult, ALU.mult)

    def h_rows(l, r0, n):
        if l > 0:
            return [(H[l][r0:r0 + n, :], 0, n)]
        segs = []
        for (a, b, src, off) in ((0, N_META, meta, 0), (N_META, TP, xp, N_META), (TP, TT, xs, TP)):
            lo, hi = max(a, r0), min(b, r0 + n)
            if lo < hi:
                segs.append((src[lo - off:hi - off, :], lo - r0, hi - lo))
        return segs

    all_tiles = list(cfg.tiles) + [(TP, NS)]

    for l in range(L):
        with ExitStack() as ph:
            uT = p.sb("uT", [128, 8, TT], BF16, ph)
            n1b = p.sb("n1b", [128, D], F32, ph)
            bcast_row(n1b[:], norm1_w[l])
            xt = [p.sb(f"xt{i}", [128, D], F32, ph) for i in range(2)]
            ut = [p.sb(f"ut{i}", [128, D], F32, ph) for i in range(2)]
            sq = p.sb("sq", [128, D], F32, ph)
            ssq = [p.sb(f"ssq{i}", [128, 2], F32, ph) for i in range(2)]
            for ti, (r0, n) in enumerate(all_tiles):
                x = xt[ti % 2]
                u = ut[ti % 2]
                for (src, po, cnt) in h_rows(l, r0, n):
                    p.dma(x[po:po + cnt, :], src)
                rmsnorm_tile(x, n, n1b, u, ssq[ti % 2], sq)

                def ev(b0, cntb, pv3, grp, r0=r0, n=n):
                    p.copy(uT[:, b0:b0 + cntb, r0:r0 + n], pv3[:, :cntb, :n], eng="act" if (b0 // 4) % 2 == 0 else "dve")
                transpose_to(ev, u, n, [128] * 8)

            wst = [p.sb(f"wst{i}", [128, 8, 512], F32, ph) for i in range(2)]
            wbf = [p.sb(f"wbf{i}", [128, 8, 512], BF16, ph) for i in range(2)]
            ev32 = [p.sb(f"ev32_{i}", [128, 512], F32, ph) for i in range(3)]
            evbf = [p.sb(f"evbf{i}", [128, 512], BF16, ph) for i in range(3)]
            chunks = []
            for (c0, wd, kind) in ((C_Z, 2048, "tm"), (C_XBC, 3072, "fm"), (C_DT, 32, "tm"), (C_Q, 1024, "tm"), (C_K, 256, "tm"),
                                   (C_V, 256, "tm"), (C_QI, 512, "tm"), (C_KI, 64, "tm"), (C_WI, 8, "tm"), (C_GS, 1024, "tm"), (C_GA, 1024, "tm")):
                for o in range(0, wd, 512):
                    chunks.append((c0 + o, min(512, wd - o), kind))
            cnt_ev = 0
            for ci, (c0, wd, kind) in enumerate(chunks):
                st = wst[ci % 2]
                wb = wbf[ci % 2]
                p.dma(st[:, :, :wd], w_in[l, :, c0:c0 + wd].r("(kc p) n -> p kc n", p=128))
                p.copy(wb[:, :, :wd], st[:, :, :wd], eng=("pool", "dve")[ci % 2])
                if kind == "fm":
                    for cc in range(wd // 128):
                        crow = c0 - C_XBC + cc * 128
                        for t0 in range(0, TT, 512):
                            tw = min(512, TT - t0)
                            bank = PS[cnt_ev % 3]
                            for kc in range(8):
                                p.mm(bank[:, :tw], wb[:, kc, cc * 128:(cc + 1) * 128], uT[:, kc, t0:t0 + tw], start=(kc == 0), stop=(kc == 7))
                            eb = evbf[cnt_ev % 3]
                            p.copy(eb[:, :tw], bank[:, :tw], eng="act" if cnt_ev % 2 == 0 else "dve")
                            p.dma(XBT[crow:crow + 128, t0:t0 + tw], eb[:, :tw], q="pool")
                            cnt_ev += 1
                    tm_tiles = [all_tiles[NT - 1], all_tiles[NT]]
                else:
                    tm_tiles = all_tiles
                for (r0, n) in tm_tiles:
                    bank = PS[cnt_ev % 3]
                    for kc in range(8):
                        p.mm(bank[:n, :wd], uT[:, kc, r0:r0 + n], wb[:, kc, :wd], start=(kc == 0), stop=(kc == 7))
                    eb = ev32[cnt_ev % 3]
                    p.copy(eb[:n, :wd], bank[:n, :wd], eng="act" if cnt_ev % 2 == 0 else "dve")
                    p.dma(PTM[r0:r0 + n, c0:c0 + wd], eb[:n, :wd], q="pool")
                    for (cs, o_p, o_s) in ((C_K, pk, sk), (C_V, pv, sv), (C_KI, pki, ski)):
                        if c0 == cs:
                            if r0 < TP:
                                p.dma(o_p[l, r0:r0 + n, :], eb[:n, :wd], q="pool")
                            else:
                                p.dma(o_s[l, :, :], eb[:n, :wd], q="pool")
                    cnt_ev += 1
            p.barrier()
            p.dma(pconv[l, :, :], PTM[TP - 3:TP, C_XBC:C_XBC + CONV_DIM], q="sp")
            for b in range(cfg.nbs):
                r = TP + b * cfg.ds + cfg.ds - 3
                p.dma(sconv[l, b, :, :], PTM[r:r + 3, C_XBC:C_XBC + CONV_DIM], q="sp")
        p.barrier()

        with ExitStack() as ph:
            wssm = p.sb("wssm", [128, 16, D], BF16, ph)
            stg = [p.sb(f"stg{i}", [128, 1024], F32, ph) for i in range(2)]
            load_w_bf16(wssm, lambda kc, n0, w: w_ssm[l, kc * 128:(kc + 1) * 128, n0:n0 + w], 16, D, stg)
            wcol = p.sb("wcol", [128, 24, 4], F32, ph)
            for k in range(4):
                p.dma(wcol[:, :, k], conv_w[l, k].r("(cc p) -> p cc", p=128), slow=True)
            bcol = p.sb("bcol", [128, 24], F32, ph)
            p.dma(bcol[:], conv_b[l:l + 1, :].r("o (cc p) -> p (o cc)", p=128), slow=True)
            browb = p.sb("browb", [1, CONV_DIM], BF16, ph)
            for o in range(0, CONV_DIM, 1024):
                p.dma(stg[0][:1, :], conv_b[l:l + 1, o:o + 1024])
                p.copy(browb[:1, o:o + 1024], stg[0][:1, :])
            cdiag = p.sb("cdiag", [128, 96, 128], BF16, ph)
            for cc in range(24):
                for k in range(4):
                    p.ts(cdiag[:, cc * 4 + k, :], identf[:], wcol[:, cc, k:k + 1], ALU.mult, eng="dve" if (cc + k) % 2 else "pool")
            A_b = p.sb("A_b", [128, 32], F32, ph)
            bcast_row(A_b[:], a_log[l])
            p.act(A_b[:], A_b[:], AF.Exp)
            p.ts(A_b[:], A_b[:], -1.0, ALU.mult)
            dtb_b = p.sb("dtb_b", [128, 32], F32, ph)
            bcast_row(dtb_b[:], dt_bias[l])
            D32 = p.sb("D32", [128, 32], F32, ph)
            bcast_row(D32[:], d_skip[l])
            D_b = p.sb("D_b", [128, 32, 64], F32, ph)
            p.copy(D_b[:], D32[:].us(2).bc([128, 32, 64]))
            snw_b = p.sb("snw_b", [128, D_INNER], F32, ph)
            bcast_row(snw_b[:], ssm_norm_w[l])
            hT = p.sb("hT", [128, D_INNER], F32, ph)
            hTb = p.sb("hTb", [128, D_INNER], BF16, ph)
            xw = p.sb("xw", [128, 24, 131], BF16, ph)
            pre32 = p.sb("pre32", [128, 24, 3], F32, ph)
            x_tm = p.sb("x_tm", [128, D_INNER], F32, ph)
            B_tm = p.sb("B_tm", [128, 512], BF16, ph)
            BT = p.sb("BT", [128, 4, 128], BF16, ph)
            CT = p.sb("CT", [128, 4, 128], BF16, ph)
            sm = p.sb("sm", [128, 8, 32], F32, ph)
            dtAb = p.sb("dtAb", [128, 32, 128], F32, ph)
            explast = p.sb("explast", [128, 32], F32, ph)
            xdt = p.sb("xdt", [128, D_INNER], BF16, ph)
            xdtw = p.sb("xdtw", [128, D_INNER], BF16, ph)
            cbT = p.sb("cbT", [128, 4, 128], F32, ph)
            arg = [p.sb(f"arg{i}", [128, 4, 128], F32, ph) for i in range(2)]
            GT = [p.sb(f"GT{i}", [128, 4, 128], BF16, ph) for i in range(2)]
            zt = p.sb("zt", [128, D_INNER], F32, ph)
            yts = [p.sb(f"yt{i}", [128, D_INNER], F32, ph) for i in range(2)]
            t1 = p.sb("t1", [128, 512], F32, ph)
            gss = p.sb("gss", [128, 8], F32, ph)
            yT = p.sb("yT", [128, 16, 128], BF16, ph)
            gs = p.sb("gs", [128, D], F32, ph)
            mso = p.sb("mso", [128, D], F32, ph)
            st_t = p.sb("st_t", [128, 128], F32, ph)

            def ssd_pre(r0, n, first, prefix_src):
                if first and prefix_src is None:
                    p.memset(xw[:, :, 0:3], 0.0)
                    p.dma(xw[:, :, 3:3 + n], XBT[:, r0:r0 + n].r("(cc p) t -> p cc t", p=128), slow=True)
                elif prefix_src is not None:
                    for r3 in range(3):
                        p.dma(pre32[:, :, r3], prefix_src[r3].r("(cc p) -> p cc", p=128), slow=True)
                    p.copy(xw[:, :, 0:3], pre32[:])
                    p.dma(xw[:, :, 3:3 + n], XBT[:, r0:r0 + n].r("(cc p) t -> p cc t", p=128), slow=True)
                else:
                    p.dma(xw[:, :, 0:3 + n], XBT[:, r0 - 3:r0 + n].r("(cc p) t -> p cc t", p=128), slow=True)
                p.dma(sm[:n, 0, :], PTM[r0:r0 + n, C_DT:C_DT + 32])
                for bk in range(4):
                    bank = PS[bk % 3]
                    for c4 in range(4):
                        cc = bk * 4 + c4
                        for k in range(4):
                            p.mm(bank[:n, c4 * 128:(c4 + 1) * 128], xw[:, cc, k:k + n], cdiag[:, cc * 4 + k, :], start=(k == 0), stop=False)
                        p.mm(bank[:n, c4 * 128:(c4 + 1) * 128], onesb[:1, :n], browb[:1, cc * 128:(cc + 1) * 128], start=False, stop=True)
                    p.act(x_tm[:n, bk * 512:(bk + 1) * 512], bank[:n, :], AF.Silu)
                bank = PS[1]
                for c4 in range(4):
                    cc = 16 + c4
                    for k in range(4):
                        p.mm(bank[:n, c4 * 128:(c4 + 1) * 128], xw[:, cc, k:k + n], cdiag[:, cc * 4 + k, :], start=(k == 0), stop=False)
                    p.mm(bank[:n, c4 * 128:(c4 + 1) * 128], onesb[:1, :n], browb[:1, cc * 128:(cc + 1) * 128], start=False, stop=True)
                p.act(B_tm[:n, :], bank[:n, :], AF.Silu)
                for which, dst in ((16, BT), (20, CT)):
                    bank = PS[2] if which == 16 else PS[0]
                    for g in range(4):
                        cc = which + g
                        for k in range(4):
                            p.mm(bank[:, g * 128:g * 128 + n], cdiag[:, cc * 4 + k, :], xw[:, cc, k:k + n], start=(k == 0), stop=(k == 3))
                    for g in range(4):
                        p.act(dst[:, g, :n], bank[:, g * 128:g * 128 + n], AF.Silu, bias=bcol[:, which + g:which + g + 1])
                dtraw, dt, dtA, acum, Ee, wj, dtw, tmp = [sm[:n, i, :] for i in range(8)]
                p.tt(dt, dtraw, dtb_b[:n, :], ALU.add)
                p.act(dt, dt, AF.Exp)
                p.act(dt, dt, AF.Ln, bias=1.0)
                p.tt(dtA, dt, A_b[:n, :], ALU.mult)
                for h in range(32):
                    if h % 2 == 0:
                        p.act(dtAb[:n, h, :n], tri[:n, :n], AF.Copy, scale=dtA[:, h:h + 1])
                    else:
                        p.ts(dtAb[:n, h, :n], tri[:n, :n], dtA[:, h:h + 1], ALU.mult)
                bank = PS[3]
                p.mm(bank[:n, 0:32], tri[:n, :n], dtA, start=True, stop=True)
                p.mm(bank[:, 32:64], onesf[:n, :], dtA, start=True, stop=True)
                p.copy(acum, bank[:n, 0:32])
                p.act(Ee, bank[:n, 0:32], AF.Exp)
                p.act(explast[:], bank[:, 32:64], AF.Exp)
                p.tt(tmp, bank[:n, 32:64], acum, ALU.subtract)
                p.act(wj, tmp, AF.Exp)
                p.tt(dtw, dt, wj, ALU.mult)
                x3 = x_tm[:n, :].r("p (h d) -> p h d", h=32)
                p.tt(xdt[:n, :].r("p (h d) -> p h d", h=32), x3, dt.us(2).bc([n, 32, 64]), ALU.mult)
                p.tt(xdtw[:n, :].r("p (h d) -> p h d", h=32), x3, dtw.us(2).bc([n, 32, 64]), ALU.mult)
                bank = PS[4]
                for g in range(4):
                    p.mm(bank[:n, g * 128:g * 128 + n], BT[:, g, :n], CT[:, g, :n], start=True, stop=True)
                p.tt(cbT[:n, :, :n], bank[:n, :].r("p (g i) -> p g i", g=4)[:, :, :n], tri[:n, :n].us(1).bc([n, 4, n]), ALU.mult)

            def ssd_main(r0, n, yt):
                dtraw, dt, dtA, acum, Ee, wj, dtw, tmp = [sm[:n, i, :] for i in range(8)]
                blocks = [(g, q4) for g in range(4) for q4 in range(2)]

                def decay(bi):
                    g, q4 = blocks[bi]
                    ab = PS[bi % 3]
                    for hh in range(4):
                        h = g * 8 + q4 * 4 + hh
                        p.mm(ab[:n, hh * 128:hh * 128 + n], lstr[:n, :n], dtAb[:n, h, :n], start=True, stop=True)
                    p.act(arg[bi % 2][:n, :, :n], ab[:n, :].r("p (a b) -> p a b", a=4)[:, :, :n], AF.Exp)
                    p.tt(GT[bi % 2][:n, :, :n], arg[bi % 2][:n, :, :n], cbT[:n, g:g + 1, :n].bc([n, 4, n]), ALU.mult)

                decay(0)
                for bi, (g, q4) in enumerate(blocks):
                    if bi + 1 < len(blocks):
                        decay(bi + 1)
                    ybank = PS[5 + (g % 2)]
                    gt = GT[bi % 2]
                    for hh in range(4):
                        h = g * 8 + q4 * 4 + hh
                        p.mm(ybank[:n, (q4 * 4 + hh) * 64:(q4 * 4 + hh + 1) * 64], gt[:n, hh, :n], xdt[:n, h * 64:(h + 1) * 64], start=True, stop=True)
                    if q4 == 1:
                        ibank = PS[7]
                        p.mm(ibank[:n, :], CT[:, g, :n], hTb[:, g * 512:(g + 1) * 512], start=True, stop=True)
                        p.tt(t1[:n, :].r("p (h d) -> p h d", h=8), ibank[:n, :].r("p (h d) -> p h d", h=8), Ee[:, g * 8:(g + 1) * 8].us(2).bc([n, 8, 64]), ALU.mult)
                        p.tt(t1[:n, :], ybank[:n, :], t1[:n, :], ALU.add)
                        p.tt(yt[:n, g * 512:(g + 1) * 512], x_tm[:n, g * 512:(g + 1) * 512], D_b[:n, g * 8:(g + 1) * 8, :].r("p h d -> p (h d)"), ALU.mult, eng="pool")
                        p.tt(yt[:n, g * 512:(g + 1) * 512], yt[:n, g * 512:(g + 1) * 512], t1[:n, :], ALU.add)
                for g in range(4):
                    sbank = PS[3 + (g % 2)]
                    p.mm(sbank[:, :], B_tm[:n, g * 128:(g + 1) * 128], xdtw[:n, g * 512:(g + 1) * 512], start=True, stop=True)
                    hv = hT[:, g * 512:(g + 1) * 512]
                    p.tt(hv.r("p (h d) -> p h d", h=8), hv.r("p (h d) -> p h d", h=8), explast[:, g * 8:(g + 1) * 8].us(2).bc([128, 8, 64]), ALU.mult)
                    p.tt(hv, hv, sbank[:, :], ALU.add)
                    p.copy(hTb[:, g * 512:(g + 1) * 512], hv, eng="act")

            def ssd_post(r0, n, yt):
                p.dma(zt[:n, :], PTM[r0:r0 + n, C_Z:C_Z + D_INNER])
                p.dma(gs[:n, :], PTM[r0:r0 + n, C_GS:C_GS + D])
                p.act(zt[:n, :], zt[:n, :], AF.Silu)
                p.tt(yt[:n, :], yt[:n, :], zt[:n, :], ALU.mult)
                for g in range(4):
                    p.act(zt[:n, g * 512:(g + 1) * 512], yt[:n, g * 512:(g + 1) * 512], AF.Square, accum=gss[:n, g:g + 1])
                p.ts(gss[:n, 4:8], gss[:n, 0:4], 1.0 / 512, ALU.mult, EPS, ALU.add)
                p.act(gss[:n, 4:8], gss[:n, 4:8], AF.Sqrt)
                p.recip(gss[:n, 4:8], gss[:n, 4:8])
                for g in range(4):
                    p.stt(yt[:n, g * 512:(g + 1) * 512], yt[:n, g * 512:(g + 1) * 512], gss[:n, 4 + g:5 + g], snw_b[:n, g * 512:(g + 1) * 512], ALU.mult, ALU.mult)

                def ev(b0, cntb, pv3, grp):
                    p.copy(yT[:, b0:b0 + cntb, :n], pv3[:, :cntb, :n], eng="act" if (b0 // 4) % 2 == 0 else "dve")
                transpose_to(ev, yt, n, [128] * 16)
                p.act(gs[:n, :], gs[:n, :], AF.Sigmoid)
                for c2 in range(2):
                    bank = PS[5 + c2]
                    for kc in range(16):
                        p.mm(bank[:n, :], yT[:, kc, :n], wssm[:, kc, c2 * 512:(c2 + 1) * 512], start=(kc == 0), stop=(kc == 15))
                    p.tt(mso[:n, c2 * 512:(c2 + 1) * 512], bank[:n, :], gs[:n, c2 * 512:(c2 + 1) * 512], ALU.mult)
                p.dma(MS[r0:r0 + n, :], mso[:n, :], q="pool")

            def state_out(dst):
                for kc in range(16):
                    bank = PS[3 + kc % 2]
                    p.tr(bank[:, :128], hT[:, kc * 128:(kc + 1) * 128], identf[:])
                    p.copy(st_t[:], bank[:, :128], eng="act")
                    p.dma(dst[kc * 128:(kc + 1) * 128, :], st_t[:], q="pool")

            def state_in(src):
                for kc in range(16):
                    p.dma(st_t[:], src[kc * 128:(kc + 1) * 128, :])
                    bank = PS[3 + kc % 2]
                    p.tr(bank[:, :128], st_t[:], identf[:])
                    p.copy(hT[:, kc * 128:(kc + 1) * 128], bank[:, :128], eng="act")
                p.copy(hTb[:], hT[:])

            p.memset(hT[:], 0.0)
            p.memset(hTb[:], 0.0)
            tl = cfg.tiles
            ssd_pre(tl[0][0], tl[0][1], True, None)
            ssd_main(tl[0][0], tl[0][1], yts[0])
            for ti, (r0, n) in enumerate(tl):
                if ti + 1 < len(tl):
                    ssd_pre(tl[ti + 1][0], tl[ti + 1][1], False, None)
                ssd_post(r0, n, yts[ti % 2])
                if ti + 1 < len(tl):
                    ssd_main(tl[ti + 1][0], tl[ti + 1][1], yts[(ti + 1) % 2])
            state_out(pssm[l])
            for b in range(cfg.nbs):
                state_in(st_ssm[l, b])
                ssd_pre(TP + b * cfg.ds, cfg.ds, True, st_conv[l, b])
                ssd_main(TP + b * cfg.ds, cfg.ds, yts[b % 2])
                ssd_post(TP + b * cfg.ds, cfg.ds, yts[b % 2])
                state_out(sssm[l, b])
        p.barrier()

        SC = HD ** -0.5
        CW = (IDX_DIM ** -0.5) * (IDX_HEADS ** -0.5)

        def attention_phase(ph, NK, NKT, key_tiles_fill, q_tiles, NQ, NR):
            NB2 = 2 if len(q_tiles) > 1 else 1
            KT = p.sb("KT", [65, 4, NK], BF16, ph)
            VA = p.sb("VA", [128, NKT, 4, 65], BF16, ph)
            kiT2 = p.sb("kiT2", [128, NK], BF16, ph)
            kmx = p.sb("kmx", [128, 4], F32, ph)
            kmaxs = p.sb("kmaxs", [128, 16], F32, ph)
            kaug = p.sb("kaug", [128, 4, 65], F32, ph)
            ksq = p.sb("ksq", [128, 256], F32, ph)
            kn = p.sb("kn", [128, 8], F32, ph)
            kvl = [(p.sb(f"kl{i}", [128, 256], F32, ph), p.sb(f"vl{i}", [128, 256], F32, ph)) for i in range(3)]
            kil = [p.sb(f"kil{i}", [128, 128], F32, ph) for i in range(3)]
            Irow = p.sb("Irow", [128, max(NK, D)], F32, ph)
            junk = p.sb("junk", [128, NK], U8, ph)
            Rh = [p.sb(f"Rh{i}", [128, 8, 512], BF16, ph) for i in range(NR)]
            mbTs = [p.sb(f"mbT{i}", [128, NKT, NQ], BF16, ph) for i in range(NB2)]
            qt = p.sb("qt", [128, D], F32, ph)
            qaug = p.sb("qaug", [128, 16, 65], F32, ph)
            QTs = [p.sb(f"QT{i}", [65, 16, NQ], BF16, ph) for i in range(NB2)]
            qil = p.sb("qil", [128, 584], F32, ph)
            qiT = p.sb("qiT", [128, 4, NQ], BF16, ph)
            diagw = p.sb("diagw", [128, 8, NQ], BF16, ph)
            bs = p.sb("bs", [128, 16], F32, ph)
            halfs = p.sb("halfs", [128, NBIS + 1], F32, ph)
            qn = p.sb("qn", [128, 16], F32, ph)
            PT = [p.sb(f"PT{i}", [128, 512], BF16, ph) for i in range(3)]
            rs = p.sb("rs", [128, 16], F32, ph)
            yatt = p.sb("yatt", [128, D], F32, ph)

            p.memset(VA[:], 1.0)
            p.memset(kaug[:], 1.0)
            p.memset(kmx[:], 0.0)

            def fill(j, col, n, ksrc, vsrc, kisrc):
                p.copy(kaug[:n, :, 0:64], ksrc.r("p (g d) -> p g d", g=4))

                def evk(b0, cntb, pv3, grp):
                    p.copy(KT[:65, :, col:col + n], pv3[:65, :4, :n], eng="act")
                transpose_to(evk, kaug[:, :, :].r("p g d -> p (g d)"), n, [65] * 4)
                p.tt(ksq[:n, :], ksrc, ksrc, ALU.mult)
                p.reduce(kn[:n, 0:4], ksq[:n, :].r("p (g d) -> p g d", g=4), ALU.add)
                p.tt(kmx[:n, :], kmx[:n, :], kn[:n, 0:4], ALU.max)
                p.copy(VA[:n, j, :, 0:64], vsrc.r("p (g d) -> p g d", g=4), eng="act")

                def evi(b0, cntb, pv3, grp):
                    p.copy(kiT2[:, col:col + n], pv3[:, 0, :n], eng="act")
                transpose_to(evi, kisrc, n, [128])

            key_tiles_fill(fill, kvl, kil)

            bank = PS[3]
            p.tr(bank[:4, :128], kmx[:, :], identf[:])
            p.reduce(kn[:4, 4:5], bank[:4, :128], ALU.max)
            p.act(kn[:4, 4:5], kn[:4, 4:5], AF.Sqrt)
            p.ts(kn[:4, 5:6], kn[:4, 4:5], -SC, ALU.mult)
            p.ts(ksq[:4, 0:16], sel4[:, :], kn[:4, 5:6], ALU.mult)
            bank = PS[4]
            p.mm(bank[:, 0:16], onesf[:4, :], ksq[:4, 0:16], start=True, stop=True)
            p.copy(kmaxs[:], bank[:, 0:16])

            def stageA1(qd, slot):
                n = qd["n"]
                ktl = qd["ktiles"]
                end = ktl[-1][1] + ktl[-1][2]
                topk = qd["topk"]
                r0 = qd["r0"]
                mbT = mbTs[slot]
                QT = QTs[slot]
                p.dma(qt[:n, :], PTM[r0:r0 + n, C_Q:C_Q + D])
                p.dma(qil[:n, :], PTM[r0:r0 + n, C_QI:C_QI + 584])
                p.tt(Irow[:n, 0:D], qt[:n, :], qt[:n, :], ALU.mult, eng="pool")
                p.reduce(qn[:n, :], Irow[:n, 0:D].r("p (h d) -> p h d", h=16), ALU.add)
                p.act(qn[:n, :], qn[:n, :], AF.Sqrt)
                p.tt(qaug[:n, :, 64:65].r("p h o -> p (h o)"), qn[:n, :], kmaxs[:n, :], ALU.mult)
                p.ts(qaug[:n, :, 0:64], qt[:n, :].r("p (h d) -> p h d", h=16), SC, ALU.mult, eng="pool")
                p.ts(bs[:n, 8:16], qil[:n, 576:584], CW, ALU.mult)
                for h in range(8):
                    p.ts(diagw[:n, h, :n], identf[:n, :n], bs[:n, 8 + h:9 + h], ALU.mult, eng="pool" if h % 2 else "dve")

                def evq(b0, cntb, pv3, grp):
                    p.copy(qiT[:, b0:b0 + cntb, :n], pv3[:, :cntb, :n], eng="act")
                transpose_to(evq, qil, n, [128] * 4, banks=(4,))
                for ci, c0 in enumerate(range(0, end, 512)):
                    w = min(512, end - c0)
                    R = Rh[ci % NR]
                    for h in range(8):
                        xb = PS[2 + h % 2]
                        po = 64 * (h % 2)
                        p.mm(xb[:n, :w], qiT[po:po + 64, h // 2, :n], kiT2[po:po + 64, c0:c0 + w], start=True, stop=True)
                        p.act(R[:n, h, :w], xb[:n, :w], AF.Relu)
                    ib = PS[4]
                    for h in range(8):
                        p.mm(ib[:n, :w], diagw[:n, h, :n], R[:n, h, :w], start=(h == 0), stop=(h == 7))
                    p.copy(Irow[:n, c0:c0 + w], ib[:n, :w], eng="dve")
            def stageA2(qd, slot):
                n = qd["n"]
                ktl = qd["ktiles"]
                end = ktl[-1][1] + ktl[-1][2]
                topk = qd["topk"]
                p.reduce(bs[:n, 0:1], Irow[:n, :end], ALU.max, absval=True)
                dt_i, dcol, dn = ktl[-1]
                p.tt(Irow[:n, dcol:dcol + dn], Irow[:n, dcol:dcol + dn], cmask[:n, :dn], ALU.add)
                p.ts(bs[:n, 1:2], bs[:n, 0:1], -1.0001, ALU.mult, -1e-6, ALU.add)
                if end > topk:
                    p.ts(bs[:n, 2:3], bs[:n, 1:2], -2.0, ALU.mult)
                    p.tt(halfs[:n, :], pow2[:n, :], bs[:n, 2:3].bc([n, NBIS + 1]), ALU.mult)
                    p.tt(bs[:n, 3:4], bs[:n, 1:2], halfs[:n, 0:1], ALU.add)
                    for k in range(NBIS):
                        p.ts(junk[:n, :end], Irow[:n, :end], bs[:n, 3:4], ALU.is_ge, None, ALU.add, accum=bs[:n, 4:5])
                        p.stt(bs[:n, 5:6], bs[:n, 4:5], topk - 0.5, halfs[:n, k:k + 1], ALU.is_ge, ALU.mult)
                        p.stt(bs[:n, 3:4], bs[:n, 5:6], bs[:n, 3:4], halfs[:n, k + 1:k + 2], ALU.add, ALU.subtract)
                    p.tt(bs[:n, 1:2], bs[:n, 3:4], halfs[:n, NBIS:NBIS + 1], ALU.subtract)

            def stageA3(qd, slot):
                n = qd["n"]
                ktl = qd["ktiles"]
                end = ktl[-1][1] + ktl[-1][2]
                mbT = mbTs[slot]
                QT = QTs[slot]
                p.ts(Irow[:n, :end], Irow[:n, :end], bs[:n, 1:2], ALU.is_ge, -1.0, ALU.add)
                for b0 in range(0, len(ktl), 4):
                    grp = ktl[b0:b0 + 4]
                    bank = PS[4]
                    pv3 = bank[:].r("p (a b) -> p a b", a=4)
                    for gi, (tj, col, nk) in enumerate(grp):
                        p.tr(pv3[:nk, gi, :n], Irow[:n, col:col + nk], identf[:n, :n])
                    if all(g[2] == 128 for g in grp) and grp[-1][0] - grp[0][0] == len(grp) - 1:
                        p.ts(mbT[:, grp[0][0]:grp[0][0] + len(grp), :n], pv3[:, :len(grp), :n], -NEG, ALU.mult, eng="dve")
                    else:
                        for gi, (tj, col, nk) in enumerate(grp):
                            p.ts(mbT[:nk, tj, :n], pv3[:nk, gi, :n], -NEG, ALU.mult, eng="dve")

                def evQ(b0, cntb, pv3, grp):
                    p.copy(QT[:65, b0:b0 + cntb, :n], pv3[:65, :cntb, :n], eng="act")
                transpose_to(evQ, qaug[:, :, :].r("p h d -> p (h d)"), n, [65] * 16, banks=(4,))

            def stageB(qd, slot):
                n = qd["n"]
                ktl = qd["ktiles"]
                mbT = mbTs[slot]
                QT = QTs[slot]
                for bk in range(3):
                    p.mm(PS[5 + bk][:n, 0:390], zerob[:1, :n], zerob[:1, 0:390], start=True, stop=False, skip=True)
                units = [(ji, g) for ji in range(len(ktl)) for g in range(4)]

                def qk(u):
                    ji, g = units[u]
                    tj, col, nk = ktl[ji]
                    sb_ = PS[u % 2]
                    p.mm(sb_[:nk, :4 * n].r("p (a b) -> p a b", a=4), KT[:65, g, col:col + nk], QT[:65, 4 * g:4 * g + 4, :n], start=True, stop=False)
                    p.mm(sb_[:nk, :4 * n].r("p (a b) -> p a b", a=4), identb[:nk, :nk], mbT[:nk, tj:tj + 1, :n].bc([nk, 4, n]), start=False, stop=True)

                qk(0)
                for u in range(len(units)):
                    if u + 1 < len(units):
                        qk(u + 1)
                    ji, g = units[u]
                    tj, col, nk = ktl[ji]
                    last = ji == len(ktl) - 1
                    pt = PT[u % 3]
                    p.act(pt[:nk, :4 * n], PS[u % 2][:nk, :4 * n], AF.Exp)
                    for hh in range(4):
                        h = 4 * g + hh
                        p.mm(PS[5 + h // 6][:n, (h % 6) * 65:(h % 6) * 65 + 65], pt[:nk, hh * n:(hh + 1) * n], VA[:nk, tj, g, :], start=False, stop=last, skip=True)
                for bk in range(3):
                    nh = 6 if bk < 2 else 4
                    ov = PS[5 + bk][:n, 0:nh * 65].r("p (h d) -> p h d", h=nh)
                    p.recip(rs[:n, bk * 6:bk * 6 + nh].us(2), ov[:, :, 64:65])
                    p.tt(yatt[:n, bk * 384:bk * 384 + nh * 64].r("p (h d) -> p h d", h=nh), ov[:, :, 0:64], rs[:n, bk * 6:bk * 6 + nh].us(2).bc([n, nh, 64]), ALU.mult)

            stageA1(q_tiles[0], 0)
            stageA2(q_tiles[0], 0)
            stageA3(q_tiles[0], 0)
            for i, qd in enumerate(q_tiles):
                nxt = i + 1 < len(q_tiles)
                if nxt:
                    stageA1(q_tiles[i + 1], (i + 1) % 2)
                    stageA2(q_tiles[i + 1], (i + 1) % 2)
                stageB(qd, i % 2)
                if nxt:
                    stageA3(q_tiles[i + 1], (i + 1) % 2)
                qd["sink"](yatt, qd["n"], qd)
            return yatt

        for b in range(cfg.nbs):
            with ExitStack() as ph:
                NKs = cfg.past + cfg.ds
                NKTs = cfg.n_pages + 1
                ptb = p.sb("ptb", [128, cfg.n_pages], I32, ph)
                ptf = p.sb("ptf", [128, cfg.n_pages], F32, ph)
                iot = p.sb("iot", [128, 1], F32, ph)
                idx = p.sb("idx", [128, cfg.n_pages], U32, ph)
                bcast_row(ptb[:], ptab[b])
                p.op("pool", lambda e: e.iota(iot.t[:], pattern=[[0, 1]], base=0, channel_multiplier=1, allow_small_or_imprecise_dtypes=True), [], [iot])
                p.copy(ptf[:], ptb[:])
                p.ts(ptf[:], ptf[:], float(PAGE), ALU.mult, iot[:, 0:1], ALU.add)
                if l > 0:
                    p.ts(ptf[:], ptf[:], float(l * cfg.n_pool * PAGE), ALU.add)
                p.copy(idx[:], ptf[:])

                def fill_s(fill, kvl, kil, b=b, idx=idx):
                    for j in range(cfg.n_pages):
                        kb, vb = kvl[j % 3]
                        ki = kil[j % 3]
                        for (dstv, pool_t) in ((kb[:, :], cache_k), (vb[:, :], cache_v), (ki[:, 0:64], cache_ki)):
                            p.dma(dstv, pool_t[:], q="pool", fn=lambda e, dstv=dstv, pool_t=pool_t, j=j: e.indirect_dma_start(
                                out=dstv.ap, out_offset=None, in_=pool_t.t[:], in_offset=bass.IndirectOffsetOnAxis(ap=idx.t[:, j:j + 1], axis=0)),
                                  extra_reads=[idx])
                        p.copy(ki[:, 64:128], ki[:, 0:64], eng="act")
                        fill(j, j * PAGE, PAGE, kb[:, :], vb[:, :], ki[:, :])
                    r = TP + b * cfg.ds
                    kb, vb = kvl[cfg.n_pages % 3]
                    ki = kil[cfg.n_pages % 3]
                    p.dma(kb[:cfg.ds, :], PTM[r:r + cfg.ds, C_K:C_K + 256])
                    p.dma(vb[:cfg.ds, :], PTM[r:r + cfg.ds, C_V:C_V + 256])
                    p.dma(ki[:cfg.ds, 0:64], PTM[r:r + cfg.ds, C_KI:C_KI + 64])
                    p.copy(ki[:cfg.ds, 64:128], ki[:cfg.ds, 0:64], eng="act")
                    fill(cfg.n_pages, cfg.past, cfg.ds, kb[:cfg.ds, :], vb[:cfg.ds, :], ki[:cfg.ds, :])

                def sink_s(yatt, n, qd, b=b):
                    p.dma(YAS[b * cfg.ds:(b + 1) * cfg.ds, :], yatt[:n, :], q="pool")

                ktl = [(j, j * PAGE, PAGE) for j in range(cfg.n_pages)] + [(cfg.n_pages, cfg.past, cfg.ds)]
                attention_phase(ph, NKs, NKTs, fill_s, [dict(n=cfg.ds, ktiles=ktl, topk=cfg.topk_s, r0=TP + b * cfg.ds, sink=sink_s)], cfg.ds, 1)
            p.barrier()

        with ExitStack() as ph:
            wat = p.sb("wat", [128, 8, D], BF16, ph)
            wou = p.sb("wou", [128, 8, D], BF16, ph)
            with ExitStack() as ph2:
                stg = [p.sb(f"stgb{i}", [128, 1024], F32, ph2) for i in range(2)]
                load_w_bf16(wat, lambda kc, n0, w: w_attn[l, kc * 128:(kc + 1) * 128, n0:n0 + w], 8, D, stg)
                load_w_bf16(wou, lambda kc, n0, w: w_out[l, kc * 128:(kc + 1) * 128, n0:n0 + w], 8, D, stg)
                p.barrier()
            n2b = p.sb("n2b", [128, D], F32, ph)
            bcast_row(n2b[:], norm2_w[l])
            ga = p.sb("ga", [128, D], F32, ph)
            msl = p.sb("msl", [128, D], F32, ph)
            hl = p.sb("hl", [128, D], F32, ph)
            aT = p.sb("aT", [128, 8, 128], BF16, ph)
            ssq2 = p.sb("ssq2", [128, 2], F32, ph)
            u2Tt = p.sb("u2Tt", [128, 8, 128], BF16, ph)

            def post_chain(yatt, n, qd):
                r0 = qd["r0"]
                p.dma(ga[:n, :], PTM[r0:r0 + n, C_GA:C_GA + D])
                p.dma(msl[:n, :], MS[r0:r0 + n, :])
                for (src, po, cnt) in h_rows(l, r0, n):
                    p.dma(hl[po:po + cnt, :], src)

                def ev(b0, cntb, pv3, grp):
                    p.copy(aT[:, b0:b0 + cntb, :n], pv3[:, :cntb, :n], eng="act")
                transpose_to(ev, yatt, n, [128] * 8)
                p.act(ga[:n, :], ga[:n, :], AF.Sigmoid)
                for c2 in range(2):
                    bank = PS[c2]
                    for kc in range(8):
                        p.mm(bank[:n, :], aT[:, kc, :n], wat[:, kc, c2 * 512:(c2 + 1) * 512], start=(kc == 0), stop=(kc == 7))
                    p.tt(ga[:n, c2 * 512:(c2 + 1) * 512], bank[:n, :], ga[:n, c2 * 512:(c2 + 1) * 512], ALU.mult)
                p.tt(msl[:n, :], msl[:n, :], ga[:n, :], ALU.add)
                transpose_to(ev, msl, n, [128] * 8)
                for c2 in range(2):
                    bank = PS[c2]
                    for kc in range(8):
                        p.mm(bank[:n, :], aT[:, kc, :n], wou[:, kc, c2 * 512:(c2 + 1) * 512], start=(kc == 0), stop=(kc == 7))
                    p.tt(hl[:n, c2 * 512:(c2 + 1) * 512], bank[:n, :], hl[:n, c2 * 512:(c2 + 1) * 512], ALU.add)
                p.dma(HM[r0:r0 + n, :], hl[:n, :], q="pool")
                rmsnorm_tile(hl, n, n2b, msl, ssq2, ga)

                def ev2(b0, cntb, pv3, grp):
                    p.copy(u2Tt[:, b0:b0 + cntb, :n], pv3[:, :cntb, :n], eng="act")
                transpose_to(ev2, msl, n, [128] * 8)
                p.dma(U2T[:, :, r0:r0 + n].r("kc p t -> p kc t"), u2Tt[:, :, :n], q="pool", slow=True)

            def fill_p(fill, kvl, kil):
                for j, (r0, n) in enumerate(cfg.tiles):
                    kb, vb = kvl[j % 3]
                    ki = kil[j % 3]
                    p.dma(kb[:n, :], PTM[r0:r0 + n, C_K:C_K + 256])
                    p.dma(vb[:n, :], PTM[r0:r0 + n, C_V:C_V + 256])
                    p.dma(ki[:n, 0:64], PTM[r0:r0 + n, C_KI:C_KI + 64])
                    p.copy(ki[:n, 64:128], ki[:n, 0:64], eng="act")
                    fill(j, r0, n, kb[:n, :], vb[:n, :], ki[:n, :])

            q_tiles = []
            for i, (r0, n) in enumerate(cfg.tiles):
                ktl = [(j, c, m) for j, (c, m) in enumerate(cfg.tiles[:i + 1])]
                q_tiles.append(dict(n=n, ktiles=ktl, topk=cfg.topk_p, r0=r0, sink=post_chain))
            ys_t = attention_phase(ph, TP, NT, fill_p, q_tiles, 128, 2)
            p.dma(ys_t[:NS, :], YAS[:, :])
            post_chain(ys_t, NS, dict(r0=TP))
        p.barrier()

        with ExitStack() as ph:
            wup = p.sb("wup", [128, 8, D_FF], BF16, ph)
            wdn = p.sb("wdn", [128, 32, D], BF16, ph)
            stg = [p.sb(f"stgc{i}", [128, 1024], F32, ph) for i in range(2)]
            load_w_bf16(wup, lambda kc, n0, w: w_up[l, kc * 128:(kc + 1) * 128, n0:n0 + w], 8, D_FF, stg)
            load_w_bf16(wdn, lambda kc, n0, w: w_down[l, kc * 128:(kc + 1) * 128, n0:n0 + w], 32, D, stg)
            GW = 256
            u2g = [p.sb(f"u2g{i}", [128, 8, GW], BF16, ph) for i in range(2)]
            aTm = p.sb("aTm", [128, 32, GW], BF16, ph)
            rl = [p.sb(f"rl{i}", [128, GW], F32, ph) for i in range(2)]
            hm = [p.sb(f"hm{i}", [128, D], F32, ph) for i in range(2)]
            ho = [p.sb(f"ho{i}", [128, D], F32, ph) for i in range(2)]
            last = l == L - 1
            if last:
                fnb = p.sb("fnb", [128, D], F32, ph)
                bcast_row(fnb[:], fnorm_w[0])
                sqf = p.sb("sqf", [128, D], F32, ph)
                ssqf = p.sb("ssqf", [128, 2], F32, ph)
            gi = 0
            bi = 0
            for g0 in range(0, TT, GW):
                gw = min(GW, TT - g0)
                ug = u2g[gi % 2]
                gi += 1
                p.dma(ug[:, :, :gw], U2T[:, :, g0:g0 + gw].r("kc p t -> p kc t"), slow=True)
                for fc in range(32):
                    bank = PS[fc % 3]
                    for kc in range(8):
                        p.mm(bank[:, :gw], wup[:, kc, fc * 128:(fc + 1) * 128], ug[:, kc, :gw], start=(kc == 0), stop=(kc == 7))
                    r = rl[fc % 2]
                    p.act(r[:, :gw], bank[:, :gw], AF.Relu)
                    p.tt(aTm[:, fc, :gw], r[:, :gw], r[:, :gw], ALU.mult, eng="dve" if fc % 2 else "pool")
                for s0 in range(0, gw, 128):
                    n = min(128, gw - s0)
                    r0 = g0 + s0
                    hmt = hm[bi % 2]
                    hot = ho[bi % 2]
                    bi += 1
                    p.dma(hmt[:n, :], HM[r0:r0 + n, :])
                    for c2 in range(2):
                        bank = PS[3 + c2]
                        for fc in range(32):
                            p.mm(bank[:n, :], aTm[:, fc, s0:s0 + n], wdn[:, fc, c2 * 512:(c2 + 1) * 512], start=(fc == 0), stop=(fc == 31))
                        p.tt(hot[:n, c2 * 512:(c2 + 1) * 512], bank[:n, :], hmt[:n, c2 * 512:(c2 + 1) * 512], ALU.add)
                    if not last:
                        p.dma(H[l + 1][r0:r0 + n, :], hot[:n, :], q="pool")
                    else:
                        rmsnorm_tile(hot, n, fnb, hmt, ssqf, sqf)
                        for (a, bnd, dst, off) in ((N_META, TP, y_p, N_META), (TP, TT, y_s, TP)):
                            lo, hi = max(a, r0), min(bnd, r0 + n)
                            if lo < hi:
                                p.dma(dst[lo - off:hi - off, :], hmt[lo - r0:hi - r0, :], q="pool")
        p.barrier()

    p.barrier()
    return nc, es, p


_CACHE = {}


def _run(cfg, per_core_inputs):
    key = (cfg.seq, cfg.nbs, cfg.n_pages, cfg.n_pool, cfg.depth)
    nc, es, p = build(cfg)
    res = run_bass_kernel_spmd(nc, per_core_inputs, core_ids=list(range(len(per_core_inputs))))
    return res.results


def kernel(x_prompt, x_sample, cache_k, cache_v, cache_kidx, state_ssm, state_conv, page_table,
           meta_tokens, norm1_w, w_in, conv_w, conv_b, dt_bias, a_log, d_skip, ssm_norm_w,
           w_ssm_proj, w_attn_proj, w_out, norm2_w, w_up, w_down, final_norm_w):
    f = lambda a: np.ascontiguousarray(np.asarray(a, dtype=np.float32))
    x_prompt, x_sample = f(x_prompt), f(x_sample)
    B, SEQ, _ = x_prompt.shape
    DB, DS, _ = x_sample.shape
    L = w_in.shape[0]
    n_pool = cache_k.shape[1]
    n_pages = page_table.shape[1]
    NCORE = 8
    nbs = DB // NCORE
    cfg = Cfg(seq=SEQ, nbs=nbs, dec_seq=DS, n_pages=n_pages, n_pool=n_pool, depth=L)
    ck = f(cache_k).reshape(L * n_pool * PAGE, KVD)
    cv = f(cache_v).reshape(L * n_pool * PAGE, KVD)
    cki = f(cache_kidx).reshape(L * n_pool * PAGE, IDX_DIM)
    shared = {
        "meta_tokens": f(meta_tokens), "cache_k": ck, "cache_v": cv, "cache_kidx": cki,
        "norm1_w": f(norm1_w), "w_in": f(w_in), "conv_w": f(conv_w), "conv_b": f(conv_b), "dt_bias": f(dt_bias),
        "a_log": f(a_log), "d_skip": f(d_skip), "ssm_norm_w": f(ssm_norm_w), "w_ssm_proj": f(w_ssm_proj),
        "w_attn_proj": f(w_attn_proj), "w_out": f(w_out), "norm2_w": f(norm2_w), "w_up": f(w_up), "w_down": f(w_down),
        "final_norm_w": f(final_norm_w).reshape(1, D),
    }
    ssm = f(state_ssm)
    scv = f(state_conv)
    pt = np.ascontiguousarray(np.asarray(page_table, dtype=np.int32))
    in_maps = []
    for c in range(NCORE):
        m = dict(shared)
        m["xp"] = x_prompt[c % B]
        m["xs"] = np.ascontiguousarray(x_sample[c * nbs:(c + 1) * nbs].reshape(nbs * DS, D))
        m["state_ssm"] = np.ascontiguousarray(ssm[:, c * nbs:(c + 1) * nbs].reshape(L, nbs, D_INNER, NSTATE))
        m["state_conv"] = np.ascontiguousarray(scv[:, c * nbs:(c + 1) * nbs])
        m["page_table"] = np.ascontiguousarray(pt[c * nbs:(c + 1) * nbs])
        in_maps.append(m)
    res = _run(cfg, in_maps)
    TP = cfg.tp
    y_prompt = np.stack([res[b]["y_p"] for b in range(B)])
    y_sample = np.concatenate([res[c]["y_s"].reshape(nbs, DS, D) for c in range(NCORE)])
    pk = np.stack([res[b]["pk"].reshape(L, TP, 4, HD) for b in range(B)], axis=1)
    pv = np.stack([res[b]["pv"].reshape(L, TP, 4, HD) for b in range(B)], axis=1)
    pki = np.stack([res[b]["pki"] for b in range(B)], axis=1)
    pssm = np.stack([res[b]["pssm"].reshape(L, SSM_HEADS, HD, NSTATE) for b in range(B)], axis=1)
    pconv = np.stack([res[b]["pconv"] for b in range(B)], axis=1)
    sk = np.concatenate([res[c]["sk"].reshape(L, nbs, DS, 4, HD) for c in range(NCORE)], axis=1)
    sv = np.concatenate([res[c]["sv"].reshape(L, nbs, DS, 4, HD) for c in range(NCORE)], axis=1)
    ski = np.concatenate([res[c]["ski"].reshape(L, nbs, DS, IDX_DIM) for c in range(NCORE)], axis=1)
    sssm = np.concatenate([res[c]["sssm"].reshape(L, nbs, SSM_HEADS, HD, NSTATE) for c in range(NCORE)], axis=1)
    sconv = np.concatenate([res[c]["sconv"] for c in range(NCORE)], axis=1)
    return (y_prompt, y_sample, pk, pv, pki, pssm, pconv, sk, sv, ski, sssm, sconv)
```
